# Optimizing a Trainium2 kernel written in Bass

```python
import jax, jax.numpy as jnp
from jax import lax
import numpy as np

D_MODEL = 1024
BATCH = 8
SEQ = 4096
DEPTH = 4

F_WIDTH = D_MODEL
F_GROUP_DIM = 128
F_GROUPS = F_WIDTH // F_GROUP_DIM
RET_HEADS = 4
RET_DK = D_MODEL // RET_HEADS
RET_DV = D_MODEL // RET_HEADS
RET_QK = RET_HEADS * RET_DK
RET_V = RET_HEADS * RET_DV
CHUNK = 128
ROPE_BASE = 10000.0
DECAY_OFFSET_FWD = 0.0
DECAY_OFFSET_BWD = 0.5
D_FF = 2816
CONV_WIDTH = 3
N_MOD = 6
EPS = 1e-6
IN_SIZES = (F_WIDTH, RET_QK, RET_QK, RET_V, RET_V, D_MODEL, D_MODEL)
D_IN = F_WIDTH + 2 * RET_QK + 2 * RET_V + 2 * D_MODEL

kernel_name = 'hybrid_fourier_retention_convffn_encoder'


def rmsnorm(x, g):
    xf = x.astype(jnp.float32)
    y = xf * lax.rsqrt(jnp.mean(xf * xf, axis=-1, keepdims=True) + EPS)
    return (y * g.astype(jnp.float32)).astype(x.dtype)


def decay_logs(offset):
    expo = -(5.0 + offset) - jnp.arange(RET_HEADS, dtype=jnp.float32)
    return jnp.log1p(-jnp.exp2(expo))


def fourier_mix(u):
    B, S, _ = u.shape
    ug = u.astype(jnp.float32).reshape(B, S, F_GROUPS, F_GROUP_DIM)
    y = jnp.fft.fft2(ug, axes=(1, 3), norm='ortho').real
    return y.reshape(B, S, F_WIDTH).astype(u.dtype)


def rotary(t):
    S = t.shape[1]
    half = t.shape[-1] // 2
    inv_freq = ROPE_BASE ** (-jnp.arange(half, dtype=jnp.float32) / half)
    ang = jnp.arange(S, dtype=jnp.float32)[:, None] * inv_freq[None, :]
    cos = jnp.cos(ang)[None, :, None, :]
    sin = jnp.sin(ang)[None, :, None, :]
    t1, t2 = t[..., :half], t[..., half:]
    return jnp.concatenate([t1 * cos - t2 * sin, t1 * sin + t2 * cos], axis=-1)


def chunk_retention(q, k, v, log_gamma, strict):
    B, H, S, dk = q.shape
    dv = v.shape[-1]
    n_chunks = S // CHUNK
    qc = q.reshape(B, H, n_chunks, CHUNK, dk)
    kc = k.reshape(B, H, n_chunks, CHUNK, dk)
    vc = v.reshape(B, H, n_chunks, CHUNK, dv)
    j = jnp.arange(CHUNK, dtype=jnp.float32)
    diff = j[:, None] - j[None, :]
    keep = (diff > 0) if strict else (diff >= 0)
    dmask = jnp.where(keep[None], jnp.exp(log_gamma[:, None, None] * jnp.maximum(diff, 0.0)[None]), 0.0)
    scores = jnp.einsum('bhncd,bhnld->bhncl', qc, kc) * dmask[None, :, None]
    intra = jnp.einsum('bhncl,bhnle->bhnce', scores, vc)
    q_decay = jnp.exp(log_gamma[:, None] * (j + 1.0))[None, :, :, None]
    k_decay = jnp.exp(log_gamma[:, None] * (CHUNK - 1.0 - j))[None, :, :, None]
    chunk_decay = jnp.exp(log_gamma * CHUNK)[None, :, None, None]

    def step(state, xs):
        qi, ki, vi = xs
        cross = jnp.einsum('bhcd,bhde->bhce', qi, state) * q_decay
        state = state * chunk_decay + jnp.einsum('bhcd,bhce->bhde', ki * k_decay, vi)
        return state, cross

    state0 = jnp.zeros((B, H, dk, dv), jnp.float32)
    xs = (jnp.moveaxis(qc, 2, 0), jnp.moveaxis(kc, 2, 0), jnp.moveaxis(vc, 2, 0))
    _, cross = lax.scan(step, state0, xs)
    out = intra + jnp.moveaxis(cross, 0, 2)
    return out.reshape(B, H, S, dv)


def retention_branch(q, k, v, g):
    B, S, _ = q.shape
    dt = q.dtype
    qh = rotary(q.astype(jnp.float32).reshape(B, S, RET_HEADS, RET_DK)) * (RET_DK ** -0.5)
    kh = rotary(k.astype(jnp.float32).reshape(B, S, RET_HEADS, RET_DK))
    vh = v.astype(jnp.float32).reshape(B, S, RET_HEADS, RET_DV)
    qh = jnp.transpose(qh, (0, 2, 1, 3))
    kh = jnp.transpose(kh, (0, 2, 1, 3))
    vh = jnp.transpose(vh, (0, 2, 1, 3))
    fwd = chunk_retention(qh, kh, vh, decay_logs(DECAY_OFFSET_FWD), strict=False)
    bwd = jnp.flip(chunk_retention(jnp.flip(qh, axis=2), jnp.flip(kh, axis=2), jnp.flip(vh, axis=2),
                                   decay_logs(DECAY_OFFSET_BWD), strict=True), axis=2)
    y = jnp.transpose(fwd + bwd, (0, 2, 1, 3))
    mu = jnp.mean(y, axis=-1, keepdims=True)
    var = jnp.mean(jnp.square(y - mu), axis=-1, keepdims=True)
    y = (y - mu) * lax.rsqrt(var + EPS)
    y = y.reshape(B, S, RET_V) * jax.nn.silu(g.astype(jnp.float32))
    return y.astype(dt)


def token_mix(h, w_in, w_fourier, w_ret, w_out):
    proj = h @ w_in
    offsets = np.cumsum(IN_SIZES)[:-1].tolist()
    u_f, q, k, v, g, a_f, a_r = jnp.split(proj, offsets, axis=-1)
    y_f = fourier_mix(u_f) @ w_fourier
    y_r = retention_branch(q, k, v, g) @ w_ret
    merged = jax.nn.sigmoid(a_f) * y_f + jax.nn.sigmoid(a_r) * y_r
    return merged @ w_out


def conv_ffn(h, w_up, conv_w, conv_b, w_down):
    S = h.shape[1]
    u = h @ w_up
    pad = CONV_WIDTH // 2
    up = jnp.pad(u, ((0, 0), (pad, pad), (0, 0)))
    u = sum(up[:, i:i + S] * conv_w[i] for i in range(CONV_WIDTH)) + conv_b
    a, b = jnp.split(u, 2, axis=-1)
    return (jax.nn.gelu(a, approximate=False) * b) @ w_down


def setup_inputs(seed: int = 0) -> dict:
    key = jax.random.key(seed)
    ks = jax.random.split(key, 15)

    def nrm(k, shape, scale):
        return jax.random.normal(k, shape, jnp.float32) * scale

    return {
        'x': nrm(ks[0], (BATCH, SEQ, D_MODEL), 1.0),
        'c': nrm(ks[1], (BATCH, D_MODEL), 1.0),
        'norm1_g': 1.0 + nrm(ks[2], (DEPTH, D_MODEL), 0.02),
        'norm2_g': 1.0 + nrm(ks[3], (DEPTH, D_MODEL), 0.02),
        'ada_w': nrm(ks[4], (DEPTH, D_MODEL, N_MOD * D_MODEL), 0.5 * D_MODEL ** -0.5),
        'ada_b': nrm(ks[5], (DEPTH, N_MOD * D_MODEL), 0.02),
        'w_in': nrm(ks[6], (DEPTH, D_MODEL, D_IN), D_MODEL ** -0.5),
        'w_fourier': nrm(ks[7], (DEPTH, F_WIDTH, D_MODEL), F_WIDTH ** -0.5),
        'w_ret': nrm(ks[8], (DEPTH, RET_V, D_MODEL), RET_V ** -0.5),
        'w_out': nrm(ks[9], (DEPTH, D_MODEL, D_MODEL), D_MODEL ** -0.5),
        'ffn_up': nrm(ks[10], (DEPTH, D_MODEL, 2 * D_FF), D_MODEL ** -0.5),
        'conv_w': nrm(ks[11], (DEPTH, CONV_WIDTH, 2 * D_FF), CONV_WIDTH ** -0.5),
        'conv_b': nrm(ks[12], (DEPTH, 2 * D_FF), 0.02),
        'ffn_down': nrm(ks[13], (DEPTH, D_FF, D_MODEL), D_FF ** -0.5),
        'final_g': 1.0 + nrm(ks[14], (D_MODEL,), 0.02),
    }


def reference(x, c, norm1_g, norm2_g, ada_w, ada_b, w_in, w_fourier, w_ret, w_out,
              ffn_up, conv_w, conv_b, ffn_down, final_g):
    c_act = jax.nn.silu(c)
    for l in range(DEPTH):
        mod = (c_act @ ada_w[l] + ada_b[l])[:, None, :]
        sh1, sc1, g1, sh2, sc2, g2 = jnp.split(mod, N_MOD, axis=-1)
        h = rmsnorm(x, norm1_g[l]) * (1.0 + sc1) + sh1
        x = x + g1 * token_mix(h, w_in[l], w_fourier[l], w_ret[l], w_out[l])
        h = rmsnorm(x, norm2_g[l]) * (1.0 + sc2) + sh2
        x = x + g2 * conv_ffn(h, ffn_up[l], conv_w[l], conv_b[l], ffn_down[l])
    return rmsnorm(x, final_g)
```

```python
import numpy as np
from contextlib import ExitStack
import concourse.bass as bass
import concourse.mybir as mybir
from concourse.bass_utils import run_bass_kernel_spmd

F32 = mybir.dt.float32
BF16 = mybir.dt.bfloat16
AF = mybir.ActivationFunctionType
ALU = mybir.AluOpType

S = 4096
D = 1024
NT = 8
DFF = 2816
EPS = 1e-6
ENGS = ['pe', 'act', 'dve', 'pool', 'sp']


class Op:
    __slots__ = ('eng', 'fn', 'dma', 'seq', 'signal', 'waits', 'sem', 'sigval', 'dmaval')


class Prog:
    def __init__(self):
        self.eng_ops = {e: [] for e in ENGS}
        self.last_w = {}
        self.readers = {}
        self.known = {e: {} for e in ENGS}
        self.dma_cnt = {}
        self.pending_dma = []
        self.last_compute = {e: None for e in ENGS}
        self.bank = 0

    def nextbank(self, n=1):
        if n == 2 and self.bank % 2:
            self.bank += 1
        b = self.bank % 8
        self.bank += n
        return b

    def op(self, eng, fn, reads=(), writes=(), dma=None):
        o = Op()
        o.eng = eng; o.fn = fn; o.dma = dma; o.signal = False; o.waits = []
        o.sem = None; o.sigval = 0; o.dmaval = 0
        o.seq = len(self.eng_ops[eng])
        if dma is not None:
            self.dma_cnt[dma] = self.dma_cnt.get(dma, 0) + 1
            o.dmaval = 16 * self.dma_cnt[dma]
            assert o.dmaval < 60000, dma
        raw = set(); deps = []
        for k in reads:
            w = self.last_w.get(k)
            if w is not None:
                raw.add(id(w)); deps.append(w)
        for k in writes:
            w = self.last_w.get(k)
            if w is not None:
                deps.append(w)
            rd = self.readers.get(k)
            if rd:
                deps.extend(rd.values())
        kn = self.known[eng]
        for d in deps:
            if d is o:
                continue
            if d.dma is not None:
                key = ('d', d.dma)
                if kn.get(key, 0) < d.dmaval:
                    kn[key] = d.dmaval; o.waits.append(d)
            else:
                if d.eng == eng and dma is None and id(d) not in raw:
                    continue
                key = ('e', d.eng)
                if kn.get(key, -1) < d.seq:
                    kn[key] = d.seq; d.signal = True; o.waits.append(d)
        for k in writes:
            self.last_w[k] = o; self.readers[k] = {}
        for k in reads:
            r = self.readers.setdefault(k, {})
            r[eng if dma is None else ('d', id(o))] = o
        self.eng_ops[eng].append(o)
        if dma is not None:
            self.pending_dma.append(o)
        else:
            self.last_compute[eng] = o
        return o

    def barrier(self):
        lasts = dict(self.last_compute)
        pend = self.pending_dma; self.pending_dma = []
        for e in ENGS:
            o = Op()
            o.eng = e; o.fn = None; o.dma = None; o.signal = False; o.waits = []
            o.sem = None; o.sigval = 0; o.dmaval = 0
            o.seq = len(self.eng_ops[e])
            kn = self.known[e]
            for e2, d in lasts.items():
                if d is None or e2 == e:
                    continue
                key = ('e', e2)
                if kn.get(key, -1) < d.seq:
                    kn[key] = d.seq; d.signal = True; o.waits.append(d)
            for d in pend:
                key = ('d', d.dma)
                if kn.get(key, 0) < d.dmaval:
                    kn[key] = d.dmaval; o.waits.append(d)
            self.eng_ops[e].append(o)
        self.last_w = {}; self.readers = {}

    def finalize(self, nc, es):
        n = 0
        for e in ENGS:
            cnt = 0; cur = None
            for o in self.eng_ops[e]:
                if o.dma is None and o.fn is not None and o.signal:
                    if cur is None or cnt >= 30000:
                        cur = es.enter_context(nc.semaphore(f"s_{e}_{n}")); n += 1; cnt = 0
                    cnt += 1; o.sem = cur; o.sigval = cnt
        self.dsems = {name: es.enter_context(nc.semaphore("d_" + name)) for name in self.dma_cnt}

    def emit(self, h, e):
        for o in self.eng_ops[e]:
            for d in o.waits:
                if d.dma is not None:
                    h.wait_ge(self.dsems[d.dma], d.dmaval)
                else:
                    h.wait_ge(d.sem, d.sigval)
            if o.fn is not None:
                ins = o.fn(h)
                if o.dma is not None:
                    ins.then_inc(self.dsems[o.dma], 16)
                elif o.signal:
                    ins.then_inc(o.sem, 1)


def make_consts():
    c = {}
    j = np.arange(128)
    ang = 2.0 * np.pi * ((j[:, None] * j[None, :]) % 128) / 128.0
    c['cg'] = (np.cos(ang) / np.sqrt(128.0)).astype(np.float32)
    c['sg'] = (np.sin(ang) / np.sqrt(128.0)).astype(np.float32)
    c['ident'] = np.eye(128, dtype=np.float32)
    n1 = np.arange(128); k1 = np.arange(128)
    T = np.zeros((32, 128, 3, 128), np.float32)
    for n2 in range(32):
        n = 32 * n1 + n2
        a = 2.0 * np.pi * ((n[:, None] * k1[None, :]) % 4096) / 4096.0
        T[n2, :, 0, :] = np.cos(a); T[n2, :, 1, :] = np.sin(a); T[n2, :, 2, :] = -np.sin(a)
    c['tmat'] = T
    g3c = np.zeros((128, 128), np.float32); g3s = np.zeros((128, 128), np.float32)
    for n2 in range(32):
        for k2 in range(32):
            a = 2.0 * np.pi * ((n2 * k2) % 32) / 32.0
            for b in range(4):
                g3c[n2 * 4 + b, k2 * 4 + b] = np.cos(a) / 64.0
                g3s[n2 * 4 + b, k2 * 4 + b] = -np.sin(a) / 64.0
    c['g3c'] = g3c; c['g3s'] = g3s
    half = 128
    inv_freq = (10000.0 ** (-np.arange(half, dtype=np.float32) / half)).astype(np.float32)
    angr = (np.arange(S, dtype=np.float32)[:, None] * inv_freq[None, :]).astype(np.float32)
    c['cosT'] = np.ascontiguousarray(np.cos(angr).T.astype(np.float32))
    c['sinT'] = np.ascontiguousarray(np.sin(angr).T.astype(np.float32))
    hh = np.arange(4, dtype=np.float64)
    lgf = np.log1p(-np.exp2(-(5.0 + 0.0) - hh))
    lgb = np.log1p(-np.exp2(-(5.0 + 0.5) - hh))
    cc = np.arange(128, dtype=np.float64)
    diff = cc[:, None] - cc[None, :]
    dm = np.zeros((4, 128, 128))
    for h in range(4):
        dm[h] = np.where(diff >= 0, np.exp(lgf[h] * np.maximum(diff, 0)), np.exp(lgb[h] * np.maximum(-diff, 0)))
    c['dmaskT'] = np.ascontiguousarray(np.transpose(dm, (2, 0, 1)) / 16.0).astype(np.float32).reshape(128, 512)
    tabf = np.zeros((128, 8, 128)); tabb = np.zeros((128, 8, 128))
    for ch in range(8):
        h = ch // 2
        tabf[:, ch, :] = (np.exp(lgf[h] * (cc + 1.0)) / 16.0)[None, :]
        tabb[:, ch, :] = (np.exp(lgb[h] * (128.0 - cc)) / 16.0)[None, :]
    c['tabf'] = tabf.astype(np.float32).reshape(128, 1024)
    c['tabb'] = tabb.astype(np.float32).reshape(128, 1024)
    decf = np.zeros((128, 1024)); decb = np.zeros((128, 1024))
    for h in range(4):
        decf[:, h * 256:(h + 1) * 256] = np.exp(lgf[h] * (127.0 - cc))[:, None]
        decb[:, h * 256:(h + 1) * 256] = np.exp(lgb[h] * cc)[:, None]
    c['decf'] = decf.astype(np.float32); c['decb'] = decb.astype(np.float32)
    c['_gcf'] = [float(np.exp(lgf[h] * 128.0)) for h in range(4)]
    c['_gcb'] = [float(np.exp(lgb[h] * 128.0)) for h in range(4)]
    return c


CONST_SHAPES = {'cg': [128, 128], 'sg': [128, 128], 'ident': [128, 128], 'tmat': [32, 128, 3, 128],
                'g3c': [128, 128], 'g3s': [128, 128], 'cosT': [128, S], 'sinT': [128, S],
                'dmaskT': [128, 512], 'tabf': [128, 1024], 'tabb': [128, 1024],
                'decf': [128, 1024], 'decb': [128, 1024]}


def build(layers, final, stop_after=None, dbg=()):
    nc = bass.Bass("TRN2", target_bir_lowering=False)
    C = make_consts()
    gcf = C['_gcf']; gcb = C['_gcb']
    NL = len(layers)

    def din(name, shape):
        return nc.dram_tensor(name, shape, F32, kind="ExternalInput").ap()

    xin = din("xT", [D, S])
    cv = din("cvec", [128, 8])
    SMW = NL * 48 + NL * 8 + NL * 8 + NL * 132 + NL * 44 + 8
    smalls = din("smalls", [128, SMW])
    WPC = 7168 + 3 * D + 2 * DFF + 6 * D + D
    wps = [din(f"wp{li}", [D, WPC]) for li in range(NL)]
    downs = din("downs", [NL, DFF, D])
    W = {}
    for li in range(NL):
        wpv = wps[li].rearrange("(k p) n -> p k n", p=128)
        o = [0]

        def cut(n, wpv=wpv, o=o):
            a_ = wpv[:, :, o[0]:o[0] + n]; o[0] += n; return a_
        W[li] = dict(w_in=cut(7168), w_f=cut(D), w_ret=cut(D), w_out=cut(D), up=cut(2 * DFF), ada=cut(6 * D),
                     winfT=cut(D), down=downs[li])
    c128 = din("c_128", [128, 5, 128])
    ctab = din("c_tab", [128, 512 + 4 * 1024])
    crot = din("c_rot", [128, 2, S])
    K = {'tmat': din("c_tmat", [32, 128, 3, 128]),
         'ident': c128[:, 0, :], 'cg': c128[:, 1, :], 'sg': c128[:, 2, :], 'g3c': c128[:, 3, :], 'g3s': c128[:, 4, :],
         'dmaskT': ctab[:, 0:512], 'tabf': ctab[:, 512:1536], 'tabb': ctab[:, 1536:2560],
         'decf': ctab[:, 2560:3584], 'decb': ctab[:, 3584:4608],
         'cosT': crot[:, 0, :], 'sinT': crot[:, 1, :]}
    _dmy = None
    yout = nc.dram_tensor("yT", [D, S], F32, kind="ExternalOutput").ap()

    def dscr(name, shape, dt):
        kind = "ExternalOutput" if name in dbg else "Internal"
        return nc.dram_tensor(name, shape, dt, kind=kind).ap()

    xS = dscr("xS", [D, S], F32)
    Bd = dscr("Bd", [2, 32, 128, 512], BF16)
    ZfT = dscr("ZfT", [D, S], BF16)
    yrTd = dscr("yrTd", [D, S], BF16)
    Sbd = dscr("Sbd", [32, 128, 2048], BF16)
    kTd = dscr("kTd", [16, 128, 2048], BF16)
    vd = dscr("vd", [32, 128, 1024], BF16)
    hTd = dscr("hTd", [D, S], BF16) if "hTd" in dbg else None

    def fm(ap):
        return ap.rearrange("(k p) n -> p k n", p=128)

    P = Prog()
    es = ExitStack()
    with es:
        RH = es.enter_context(nc.sbuf_tensor("RH", [128, 32768], BF16))
        RW = es.enter_context(nc.sbuf_tensor("RW", [128, 33792], BF16))
        RX = es.enter_context(nc.sbuf_tensor("RX", [128, 32768], BF16))
        RXf = RX.bitcast(F32)
        ps = es.enter_context(nc.psum_tensor("ps", [128, 8, 512], F32))
        psb = ps.bitcast(BF16)
        smallf = es.enter_context(nc.sbuf_tensor("smallf", [128, 1536], F32))
        smallb = es.enter_context(nc.sbuf_tensor("smallb", [128, 2048], BF16))
        hT = RH[:, :].rearrange("p (k n) -> p k n", k=8)

        so = [0]

        def sf(n):
            a = smallf[:, so[0]:so[0] + n]; so[0] += n; return a
        cvt = sf(8); adab_t = sf(NL * 48); n1g_t = sf(NL * 8); n2g_t = sf(NL * 8)
        cw_t = sf(NL * 132); cb_t = sf(NL * 44); fing_t = sf(8)
        modT = sf(48); prm = sf(16); eps_t = sf(1); stat = sf(32)
        assert so[0] <= 1536
        adab_v = adab_t.rearrange("p (l j) -> p l j", l=NL)
        n1g_v = n1g_t.rearrange("p (l j) -> p l j", l=NL)
        n2g_v = n2g_t.rearrange("p (l j) -> p l j", l=NL)
        cw_v = cw_t.rearrange("p (l i j) -> p l i j", l=NL, i=3)
        cb_v = cb_t.rearrange("p (l j) -> p l j", l=NL)
        bo = [0]

        def sb(n):
            a = smallb[:, bo[0]:bo[0] + n]; bo[0] += n; return a
        cact = sb(8); ones_b = sb(128); ident_b = sb(128); cg_b = sb(128); sg_b = sb(128)
        g3c_b = sb(128); g3s_b = sb(128)
        assert bo[0] <= 2048

        rx = [0]

        def rx_reset():
            rx[0] = 0

        def xb(shape):
            n = int(np.prod(shape)); o = rx[0] // 2; rx[0] += n * 2
            assert rx[0] <= 65536, rx[0]
            a = RX[:, o:o + n]
            if len(shape) == 2:
                a = a.rearrange("p (a b) -> p a b", a=shape[0])
            elif len(shape) == 3:
                a = a.rearrange("p (a b c) -> p a b c", a=shape[0], b=shape[1])
            return a

        def xf(shape):
            rx[0] = (rx[0] + 3) // 4 * 4
            n = int(np.prod(shape)); o = rx[0] // 4; rx[0] += n * 4
            assert rx[0] <= 65536, rx[0]
            a = RXf[:, o:o + n]
            if len(shape) == 2:
                a = a.rearrange("p (a b) -> p a b", a=shape[0])
            elif len(shape) == 3:
                a = a.rearrange("p (a b c) -> p a b c", a=shape[0], b=shape[1])
            return a

        def wv(off, shape):
            n = int(np.prod(shape))
            assert off + n <= 33792
            a = RW[:, off:off + n]
            if len(shape) == 2:
                a = a.rearrange("p (a b) -> p a b", a=shape[0])
            elif len(shape) == 3:
                a = a.rearrange("p (a b c) -> p a b c", a=shape[0], b=shape[1])
            return a

        def dma(eng, out, in_, reads, writes, name):
            P.op(eng, lambda h, out=out, in_=in_: h.dma_start(out=out, in_=in_), reads, writes, dma=name)

        def mmgroup(out, pairs, reads, writes):
            def fn(t, out=out, pairs=pairs):
                n = len(pairs)
                for i, (l, r) in enumerate(pairs):
                    ins = t.matmul(out, lhsT=l, rhs=r, start=(i == 0), stop=(i == n - 1))
                return ins
            P.op('pe', fn, reads, writes)

        def act(out, in_, func, reads, writes, bias=None, scale=None, accum=None):
            def fn(a, out=out, in_=in_, func=func, bias=bias, scale=scale, accum=accum):
                kw = {}
                if bias is not None: kw['bias'] = bias
                if scale is not None: kw['scale'] = scale
                if accum is not None: kw['accum_out'] = accum
                return a.activation(out=out, in_=in_, func=func, **kw)
            P.op('act', fn, reads, writes)

        def tt(eng, out, in0, in1, op, reads, writes):
            P.op(eng, lambda v, out=out, in0=in0, in1=in1, op=op: v.tensor_tensor(out=out, in0=in0, in1=in1, op=op),
                 reads, writes)

        def stt(eng, out, in0, scalar, in1, op0, op1, reads, writes):
            P.op(eng, lambda v, out=out, in0=in0, scalar=scalar, in1=in1, op0=op0, op1=op1:
                 v.scalar_tensor_tensor(out=out, in0=in0, scalar=scalar, in1=in1, op0=op0, op1=op1), reads, writes)

        def ts(eng, out, in0, s1, s2, op0, op1, reads, writes):
            def fn(v, out=out, in0=in0, s1=s1, s2=s2, op0=op0, op1=op1):
                if s2 is None:
                    return v.tensor_scalar(out=out, in0=in0, scalar1=s1, scalar2=None, op0=op0)
                return v.tensor_scalar(out=out, in0=in0, scalar1=s1, scalar2=s2, op0=op0, op1=op1)
            P.op(eng, fn, reads, writes)

        def copy(eng, out, in_, reads, writes):
            if eng == 'act':
                P.op('act', lambda a, out=out, in_=in_: a.copy(out=out, in_=in_), reads, writes)
            else:
                P.op(eng, lambda v, out=out, in_=in_: v.tensor_copy(out=out, in_=in_), reads, writes)

        def memset(eng, ap, val, writes):
            P.op(eng, lambda v, ap=ap, val=val: v.memset(ap, val), (), writes)

        def wload(dst, src, name='RW', key='RW'):
            dma('pool', dst, src, (), [key], name)

        TK = [('hT', t) for t in range(NT)]

        dma('sp', cvt, cv, (), ['small'], 'small')
        dma('sp', smallf[:, 8:8 + SMW], smalls, (), ['small'], 'small')
        for nm, dst in (('ident', ident_b), ('cg', cg_b), ('sg', sg_b), ('g3c', g3c_b), ('g3s', g3s_b)):
            dma('pool', dst, K[nm], (), ['smallb'], 'smallb')
        memset('dve', ones_b, 1.0, ['smallb'])
        memset('dve', eps_t, EPS, ['small'])
        act(cact, cvt, AF.Silu, ['small'], ['cact'])
        P.barrier()

        xsrc = xin
        for li in range(NL):
            Wl = W[li]
            for half in range(2):
                Wa = wv(0, [8, 3072])
                wload(Wa, Wl['ada'][:, :, half * 3072:(half + 1) * 3072])

                def fn(t, Wa=Wa, half=half):
                    for jj in range(24):
                        j = half * 24 + jj
                        for k in range(8):
                            ins = t.matmul(ps[:, 0, j:j + 1], lhsT=Wa[:, k, jj * 128:(jj + 1) * 128],
                                           rhs=cact[:, k:k + 1], start=(k == 0), stop=(k == 7))
                    return ins
                P.op('pe', fn, ['RW', 'cact'], [('ps', 0)])
            tt('dve', modT, ps[:, 0, 0:48], adab_v[:, li, :], ALU.add, [('ps', 0), 'small'], ['modT'])
            stt('dve', prm[:, 0:8], modT[:, 8:16], 1.0, n1g_v[:, li, :], ALU.add, ALU.mult, ['modT', 'small'], ['prm'])
            stt('dve', prm[:, 8:16], modT[:, 32:40], 1.0, n2g_v[:, li, :], ALU.add, ALU.mult, ['modT', 'small'], ['prm'])
            P.barrier()
            A1 = prm[:, 0:8]; A2 = prm[:, 8:16]
            B1 = modT[:, 0:8]; G1 = modT[:, 16:24]; B2 = modT[:, 24:32]; G2 = modT[:, 40:48]

            def phase_norm(xsrc, A, B):
                rx_reset()
                xt = [xf([8, 512]), xf([8, 512])]
                xsq = xb([8, 512])
                rs = [xf([512]), xf([512])]
                tmp = [xf([512]), xf([512])]
                xv = fm(xsrc)
                for t in range(NT):
                    b = t % 2
                    dma('sp', xt[b], xv[:, :, t * 512:(t + 1) * 512], [('x', t)], [('xt', b)], f'xt{b}')
                    for k in range(8):
                        act(xsq[:, k, :], xt[b][:, k, :], AF.Square, [('xt', b)], [('xsq', k)])
                    bk = P.nextbank()
                    mmgroup(ps[:, bk, :], [(ones_b, xsq[:, k, :]) for k in range(8)],
                            [('xsq', k) for k in range(8)], [('ps', bk)])
                    act(rs[b], ps[:, bk, :], AF.Sqrt, [('ps', bk)], [('rs', b)], bias=eps_t, scale=1.0 / D)
                    P.op('dve', lambda v, o=rs[b]: v.reciprocal(out=o, in_=o), [('rs', b)], [('rs', b)])
                    for k in range(8):
                        stt('dve', tmp[k % 2], xt[b][:, k, :], A[:, k:k + 1], rs[b], ALU.mult, ALU.mult,
                            [('xt', b), ('rs', b), 'prm'], [('tmp', k % 2)])
                        act(hT[:, k, t * 512:(t + 1) * 512], tmp[k % 2], AF.Identity,
                            [('tmp', k % 2), 'modT'], [('hT', t)], bias=B[:, k:k + 1], scale=1.0)
                P.barrier()

            phase_norm(xsrc, A1, B1)
            if hTd is not None:
                dma('sp', fm(hTd), hT, TK, ['hTd'], 'dbg')
                P.barrier()
            if stop_after == 'norm1':
                break

            rx_reset()
            Wf_b = xb([8, 1024]); WiT_b = xb([8, 1024])
            Wcs = wv(0, [8, 2048])
            M2 = wv(16384, [8, 2048])
            dma('pool', Wf_b, Wl['w_f'], (), ['Wf'], 'Wf')
            dma('pool', WiT_b, Wl['winfT'], (), ['WiT'], 'WiT')
            ev = 0
            for g in range(8):
                for cs in range(2):
                    for ot in range(2):
                        bk = P.nextbank()
                        mmgroup(ps[:, bk, :], [((cg_b if cs == 0 else sg_b), Wf_b[:, g, ot * 512:(ot + 1) * 512])],
                                ['Wf', 'smallb'], [('ps', bk)])
                        copy('act' if ev % 2 else 'dve', M2[:, g, cs * 1024 + ot * 512: cs * 1024 + (ot + 1) * 512],
                             ps[:, bk, :], [('ps', bk)], ['M2'])
                        ev += 1
            for m in range(8):
                for cs in range(2):
                    for ot in range(2):
                        bk = P.nextbank()
                        mmgroup(ps[:, bk, :],
                                [(WiT_b[:, c, m * 128:(m + 1) * 128], M2[:, c, cs * 1024 + ot * 512: cs * 1024 + (ot + 1) * 512])
                                 for c in range(8)], ['WiT', 'M2'], [('ps', bk)])
                        for q in range(2):
                            ct4 = 2 * ot + q
                            copy('act' if ev % 2 else 'dve', Wcs[:, m, ct4 * 512 + cs * 256: ct4 * 512 + (cs + 1) * 256],
                                 ps[:, bk, q * 256:(q + 1) * 256], [('ps', bk)], ['Wcs'])
                            ev += 1
            P.barrier()

            rx_reset()
            Dt = [xb([512]), xb([512])]
            Bt = [xb([512]), xb([512])]
            Tt = [xb([3, 128]), xb([3, 128])]
            Zt = xb([2, 4096])
            Zt4 = Zt.rearrange("p c (j k) -> p c j k", k=32)
            Bp = wv(16384, [32, 512])
            for ct4 in range(4):
                for n2 in range(32):
                    b = n2 % 2
                    dma('pool', Tt[b], K['tmat'][n2], (), [('Tt', b)], f'Tt{b}')
                    bk = P.nextbank()
                    mmgroup(ps[:, bk, :], [(hT[:, m, n2::32], Wcs[:, m, ct4 * 512:(ct4 + 1) * 512]) for m in range(8)],
                            TK + ['Wcs'], [('ps', bk)])
                    copy('act', Dt[b], ps[:, bk, :], [('ps', bk)], [('Dt', b)])
                    bk2 = P.nextbank()

                    def fn(t, bk2=bk2, b=b):
                        t.matmul(ps[:, bk2, 0:256], lhsT=Tt[b][:, 0, :], rhs=Dt[b][:, 0:256], start=True, stop=False)
                        t.matmul(ps[:, bk2, 0:256], lhsT=Tt[b][:, 2, :], rhs=Dt[b][:, 256:512], start=False, stop=True)
                        t.matmul(ps[:, bk2, 256:512], lhsT=Tt[b][:, 0, :], rhs=Dt[b][:, 256:512], start=True, stop=False)
                        return t.matmul(ps[:, bk2, 256:512], lhsT=Tt[b][:, 1, :], rhs=Dt[b][:, 0:256], start=False, stop=True)
                    P.op('pe', fn, [('Tt', b), ('Dt', b)], [('ps', bk2)])
                    copy('dve', Bt[b], ps[:, bk2, :], [('ps', bk2)], [('Bt', b)])
                    dma('sp', Bd[ct4 % 2, n2], Bt[b], [('Bt', b)], [('Bd', ct4 % 2, n2 // 16)], f'Bts{b}')
                Bdv = Bd[ct4 % 2].rearrange("n (a k) c -> (n a) k c", a=4)
                for hf in range(2):
                    dma('sp', Bp[:, hf * 16:(hf + 1) * 16, :], Bdv[:, hf * 16:(hf + 1) * 16, :],
                        [('Bd', ct4 % 2, 0), ('Bd', ct4 % 2, 1)], [('Bp', hf)], f'Bp{hf}')
                for chh in range(2):
                    for k0 in range(0, 32, 4):
                        bk = P.nextbank()

                        def fn(t, bk=bk, k0=k0, chh=chh):
                            for q in range(4):
                                kk = k0 + q
                                t.matmul(ps[:, bk, q * 128:(q + 1) * 128], lhsT=Bp[:, kk, chh * 128:(chh + 1) * 128],
                                         rhs=g3c_b, start=True, stop=False)
                                ins = t.matmul(ps[:, bk, q * 128:(q + 1) * 128],
                                               lhsT=Bp[:, kk, 256 + chh * 128:256 + (chh + 1) * 128],
                                               rhs=g3s_b, start=False, stop=True)
                            return ins
                        P.op('pe', fn, [('Bp', k0 // 16), 'smallb'], [('ps', bk)])
                        copy('act' if (k0 // 4) % 2 else 'dve', Zt4[:, chh, :, k0:k0 + 4],
                             ps[:, bk, :].rearrange("p (q j) -> p j q", q=4), [('ps', bk)], ['Zt'])
                dma('sp', fm(ZfT)[:, ct4 * 2:ct4 * 2 + 2, :], Zt, ['Zt'], [('ZfT', t) for t in range(NT)], 'Zts')
            P.barrier()
            if stop_after == 'fourier':
                break

            rx_reset()
            Wqkvg = wv(0, [8, 4096])
            for i in range(4):
                wload(Wqkvg[:, :, i * 1024:(i + 1) * 1024], Wl['w_in'][:, :, 1024 + i * 1024: 2048 + i * 1024],
                      key=('Wr', i))
            RT = 256; NRT = S // RT; CPT = RT // 128
            cst = [xf([RT]), xf([RT])]
            snt = [xf([RT]), xf([RT])]
            tf1 = xf([RT]); tf2 = xf([RT])
            pa = xf([RT]); pb_ = xf([RT]); pc = xf([RT]); pd = xf([RT])
            kT = xb([8, RT]); qT = xb([8, RT])
            kd = xb([1024]); v2 = [xb([1024]), xb([1024])]
            Sst = xf([4, 512])
            Sbf = xb([4, 512])
            Sbl = [xb([4, 512])] * 2
            scT = xb([512]); sg_t = xb([1024]); yn = xb([1024]); yg = yn
            yrT = xb([8, RT])
            qdf = xb([8, 128]); qdb = xb([8, 128])
            tabf_b = xb([8, 128]); tabb_b = xb([8, 128]); decf_b = xb([1024]); decb_b = xb([1024]); dmask_b = xb([512])
            junk = xb([256])
            dma('pool', tabf_b.rearrange("p a b -> p (a b)"), K['tabf'], (), ['rc'], 'rc')
            dma('pool', tabb_b.rearrange("p a b -> p (a b)"), K['tabb'], (), ['rc'], 'rc')
            dma('pool', decf_b, K['decf'], (), ['rc'], 'rc')
            dma('pool', decb_b, K['decb'], (), ['rc'], 'rc')
            dma('pool', dmask_b, K['dmaskT'], (), ['rc'], 'rc')
            s1 = stat[:, 0:4]; s2 = stat[:, 4:8]; mean = stat[:, 8:12]; msq = stat[:, 12:16]
            var = stat[:, 16:20]; rstd = stat[:, 20:24]; nbias = stat[:, 24:28]

            def rotary_proj(T, col0, dst, dkey):
                b = T % 2
                for h in range(4):
                    bks = []
                    for q in range(2):
                        ch = 2 * h + q
                        bk = P.nextbank(); bks.append(bk)
                        mmgroup(ps[:, bk, 0:RT],
                                [(Wqkvg[:, m, col0 + ch * 128: col0 + (ch + 1) * 128], hT[:, m, T * RT:(T + 1) * RT])
                                 for m in range(8)], [('hT', (T * RT) // 512), ('Wr', col0 // 1024)], [('ps', bk)])
                    copy('act', tf1, ps[:, bks[0], 0:RT], [('ps', bks[0])], ['tf1'])
                    copy('act', tf2, ps[:, bks[1], 0:RT], [('ps', bks[1])], ['tf2'])
                    tt('dve', pa, tf1, cst[b], ALU.mult, ['tf1', ('cs', b)], ['pa'])
                    tt('dve', pb_, tf2, snt[b], ALU.mult, ['tf2', ('cs', b)], ['pb'])
                    tt('dve', dst[:, 2 * h, :], pa, pb_, ALU.subtract, ['pa', 'pb'], [dkey])
                    tt('dve', pc, tf1, snt[b], ALU.mult, ['tf1', ('cs', b)], ['pc'])
                    tt('pool', pd, tf2, cst[b], ALU.mult, ['tf2', ('cs', b)], ['pd'])
                    tt('dve', dst[:, 2 * h + 1, :], pc, pd, ALU.add, ['pc', 'pd'], [dkey])

            def load_cs(T):
                b = T % 2
                dma('sp', cst[b], K['cosT'][:, T * RT:(T + 1) * RT], (), [('cs', b)], f'cs{b}')
                dma('sp', snt[b], K['sinT'][:, T * RT:(T + 1) * RT], (), [('cs', b)], f'cs{b}')

            def kv_chunk(T, c4, dec_b, fwd):
                n = T * CPT + c4
                v_t = v2[n % 2]; vk = ('v_t', n % 2)
                if not fwd:
                    bk = P.nextbank(2)
                    for ot in range(2):
                        mmgroup(ps[:, bk + ot, :],
                                [(hT[:, m, n * 128:(n + 1) * 128], Wqkvg[:, m, 2048 + ot * 512: 2048 + (ot + 1) * 512])
                                 for m in range(8)], [('hT', n // 4), ('Wr', 2)], [('ps', bk + ot)])
                    copy('act', v_t.rearrange("p (a b) -> p a b", a=2), ps[:, bk:bk + 2, :],
                         [('ps', bk), ('ps', bk + 1)], [vk])
                    dma('sp', vd[n], v_t, [vk], [('vd', n)], f'vts{n % 2}')
                else:
                    dma('sp', v_t, vd[n], [('vd', n)], [vk], f'vtl{n % 2}')
                bk = P.nextbank()

                def fn(t, bk=bk, c4=c4):
                    for j in range(8):
                        ins = t.transpose(psb[:, bk, j * 128:(j + 1) * 128], kT[:, j, c4 * 128:(c4 + 1) * 128], ident_b)
                    return ins
                P.op('pe', fn, ['kT', 'smallb'], [('ps', bk)])
                tt('dve', kd, psb[:, bk, :], dec_b, ALU.mult, [('ps', bk), 'rc'], ['kd'])
                return v_t, vk

            def state_update(gc, v_t, vk):
                for h in range(4):
                    bk = P.nextbank()

                    def fn(t, bk=bk, h=h, v_t=v_t):
                        for dc in range(2):
                            ins = t.matmul(ps[:, bk, dc * 256:(dc + 1) * 256],
                                           lhsT=kd[:, h * 256 + dc * 128: h * 256 + (dc + 1) * 128],
                                           rhs=v_t[:, h * 256:(h + 1) * 256], start=True, stop=True)
                        return ins
                    P.op('pe', fn, ['kd', vk], [('ps', bk)])
                    stt('dve', Sst[:, h, :], Sst[:, h, :], gc[h], ps[:, bk, :], ALU.mult, ALU.add,
                        [('ps', bk), 'Sst'], ['Sst'])

            memset('dve', Sst, 0.0, ['Sst'])
            for T in range(NRT - 1, -1, -1):
                load_cs(T)
                rotary_proj(T, 1024, kT, 'kT')
                dma('sp', kTd[T], kT.rearrange("p a b -> p (a b)"), ['kT'], [('kTd', T)], 'kTs')
                for c4 in range(CPT - 1, -1, -1):
                    n = T * CPT + c4
                    v_t, vk = kv_chunk(T, c4, decb_b, False)
                    copy('act', Sbl[0], Sst, ['Sst'], [('Sbl', 0)])
                    dma('sp', Sbd[n], Sbl[0].rearrange("p a b -> p (a b)"), [('Sbl', 0)], [('Sbd', n)], 'Sbs0')
                    state_update(gcb, v_t, vk)
            memset('dve', Sst, 0.0, ['Sst'])
            memset('dve', Sbf, 0.0, ['Sbf'])
            for T in range(NRT):
                load_cs(T)
                rotary_proj(T, 0, qT, 'qT')
                dma('sp', kT.rearrange("p a b -> p (a b)"), kTd[T], [('kTd', T)], ['kT'], 'kTl')
                for c4 in range(CPT):
                    n = T * CPT + c4
                    cs_ = slice(c4 * 128, (c4 + 1) * 128)
                    v_t, vk = kv_chunk(T, c4, decf_b, True)
                    dma('sp', Sbl[0].rearrange("p a b -> p (a b)"), Sbd[n], [('Sbd', n)], [('Sbl', 0)], 'Sbl0')
                    bk = P.nextbank(2)
                    for ot in range(2):
                        mmgroup(ps[:, bk + ot, :],
                                [(hT[:, m, n * 128:(n + 1) * 128], Wqkvg[:, m, 3072 + ot * 512: 3072 + (ot + 1) * 512])
                                 for m in range(8)], [('hT', n // 4), ('Wr', 3)], [('ps', bk + ot)])
                    act(sg_t.rearrange("p (a b) -> p a b", a=2), ps[:, bk:bk + 2, :], AF.Silu,
                        [('ps', bk), ('ps', bk + 1)], ['sg'])
                    bk = P.nextbank()

                    def fn(t, bk=bk, cs_=cs_):
                        for h in range(4):
                            for dc in range(2):
                                ins = t.matmul(ps[:, bk, h * 128:(h + 1) * 128], lhsT=kT[:, 2 * h + dc, cs_],
                                               rhs=qT[:, 2 * h + dc, cs_], start=(dc == 0), stop=(dc == 1))
                        return ins
                    P.op('pe', fn, ['kT', 'qT'], [('ps', bk)])
                    tt('dve', scT, ps[:, bk, :], dmask_b, ALU.mult, [('ps', bk), 'rc'], ['scT'])
                    tt('dve', qdf, qT[:, :, cs_], tabf_b, ALU.mult, ['qT', 'rc'], ['qdf'])
                    tt('dve', qdb, qT[:, :, cs_], tabb_b, ALU.mult, ['qT', 'rc'], ['qdb'])
                    bky = P.nextbank(2)

                    def fn(t, bky=bky, n=n, v_t=v_t):
                        for h in range(4):
                            o = ps[:, bky + h // 2, (h % 2) * 256:(h % 2 + 1) * 256]
                            t.matmul(o, lhsT=scT[:, h * 128:(h + 1) * 128], rhs=v_t[:, h * 256:(h + 1) * 256],
                                     start=True, stop=False)
                            for dc in range(2):
                                t.matmul(o, lhsT=qdf[:, 2 * h + dc, :], rhs=Sbf[:, h, dc * 256:(dc + 1) * 256],
                                         start=False, stop=False)
                            for dc in range(2):
                                ins = t.matmul(o, lhsT=qdb[:, 2 * h + dc, :], rhs=Sbl[0][:, h, dc * 256:(dc + 1) * 256],
                                               start=False, stop=(dc == 1))
                        return ins
                    P.op('pe', fn, ['scT', vk, 'qdf', 'qdb', 'Sbf', ('Sbl', 0)], [('ps', bky), ('ps', bky + 1)])
                    yk = [('ps', bky), ('ps', bky + 1)]
                    for h in range(4):
                        o = ps[:, bky + h // 2, (h % 2) * 256:(h % 2 + 1) * 256]
                        act(junk, o, AF.Copy, yk, ['junk', 'st1'], accum=s1[:, h:h + 1])
                        act(junk, o, AF.Square, yk, ['junk', 'st1'], accum=s2[:, h:h + 1])
                    ts('dve', mean, s1, 1.0 / 256.0, None, ALU.mult, None, ['st1', 'junk'], ['mean'])
                    tt('dve', msq, mean, mean, ALU.mult, ['mean'], ['msq'])
                    stt('dve', var, s2, 1.0 / 256.0, msq, ALU.mult, ALU.subtract, ['st1', 'junk', 'msq'], ['var'])
                    act(var, var, AF.Sqrt, ['var'], ['var'], bias=eps_t, scale=1.0)
                    P.op('dve', lambda v: v.reciprocal(out=rstd, in_=var), ['var'], ['rstd'])
                    stt('dve', nbias, mean, -1.0, rstd, ALU.mult, ALU.mult, ['mean', 'rstd'], ['nbias'])
                    for h in range(4):
                        o = ps[:, bky + h // 2, (h % 2) * 256:(h % 2 + 1) * 256]
                        act(yn[:, h * 256:(h + 1) * 256], o, AF.Identity, yk + ['rstd', 'nbias'], ['yn'],
                            bias=nbias[:, h:h + 1], scale=rstd[:, h:h + 1])
                    tt('dve', yg, yn, sg_t, ALU.mult, ['yn', 'sg'], ['yg'])
                    bk = P.nextbank()

                    def fn(t, bk=bk):
                        for j in range(8):
                            ins = t.transpose(psb[:, bk, j * 128:(j + 1) * 128], yg[:, j * 128:(j + 1) * 128], ident_b)
                        return ins
                    P.op('pe', fn, ['yg', 'smallb'], [('ps', bk)])
                    copy('act', yrT[:, :, cs_], psb[:, bk, :].rearrange("p (a b) -> p a b", a=8), [('ps', bk)], ['yrT'])
                    state_update(gcf, v_t, vk)
                    copy('act', Sbf, Sst, ['Sst'], ['Sbf'])
                dma('sp', fm(yrTd)[:, :, T * RT:(T + 1) * RT], yrT, ['yrT'], [('yrTd', T)], 'yrs')
            P.barrier()
            if stop_after == 'ret':
                break

            rx_reset()
            Wm = wv(0, [8, 4096])
            wload(Wm[:, :, 0:1024], Wl['w_in'][:, :, 5120:6144], key=('Wm', 0))
            wload(Wm[:, :, 1024:2048], Wl['w_in'][:, :, 6144:7168], key=('Wm', 1))
            wload(Wm[:, :, 2048:3072], Wl['w_ret'], key=('Wm', 2))
            wload(Wm[:, :, 3072:4096], Wl['w_out'], key=('Wm', 3))
            xt1 = xf([8, 512]); xt = [xt1, xt1]
            zf1 = xb([8, 512]); zf = [zf1, zf1]
            yr1 = xb([8, 512]); yr = [yr1, yr1]
            mg = xb([8, 512])
            saf = [xf([512]), xf([512])]; sar = [xf([512]), xf([512])]
            m1 = [xf([512]), xf([512])]; m2 = [xf([512]), xf([512])]
            xv = fm(xsrc); xo = fm(xS)
            for T in range(NT):
                b = T % 2
                tsl = slice(T * 512, (T + 1) * 512)
                dma('sp', xt[b], xv[:, :, tsl], [('x', T)], [('xt', 0)], 'xt0')
                dma('sp', zf[b], fm(ZfT)[:, :, tsl], [('ZfT', T)], [('zf', 0)], 'zf0')
                dma('sp', yr[b], fm(yrTd)[:, :, tsl], [('yrTd', 2 * T), ('yrTd', 2 * T + 1)], [('yr', 0)], 'yr0')
                for oc in range(8):
                    q = oc % 2
                    osl = slice(oc * 128, (oc + 1) * 128)
                    bk = P.nextbank()
                    mmgroup(ps[:, bk, :], [(Wm[:, m, osl], hT[:, m, tsl]) for m in range(8)],
                            [('hT', T), ('Wm', 0)], [('ps', bk)])
                    act(saf[q], ps[:, bk, :], AF.Sigmoid, [('ps', bk)], [('saf', q)])
                    bk = P.nextbank()
                    mmgroup(ps[:, bk, :], [(Wm[:, m, 1024 + oc * 128: 1024 + (oc + 1) * 128], hT[:, m, tsl]) for m in range(8)],
                            [('hT', T), ('Wm', 1)], [('ps', bk)])
                    act(sar[q], ps[:, bk, :], AF.Sigmoid, [('ps', bk)], [('sar', q)])
                    bk = P.nextbank()
                    mmgroup(ps[:, bk, :], [(Wm[:, m, 2048 + oc * 128: 2048 + (oc + 1) * 128], yr[b][:, m, :]) for m in range(8)],
                            [('yr', 0), ('Wm', 2)], [('ps', bk)])
                    tt('dve', m1[q], saf[q], zf[b][:, oc, :], ALU.mult, [('saf', q), ('zf', 0)], [('m1', q)])
                    tt('dve', m2[q], ps[:, bk, :], sar[q], ALU.mult, [('ps', bk), ('sar', q)], [('m2', q)])
                    tt('dve', mg[:, oc, :], m1[q], m2[q], ALU.add, [('m1', q), ('m2', q)], ['mg'])
                for oc in range(8):
                    bk = P.nextbank()
                    mmgroup(ps[:, bk, :], [(Wm[:, m, 3072 + oc * 128: 3072 + (oc + 1) * 128], mg[:, m, :]) for m in range(8)],
                            ['mg', ('Wm', 3)], [('ps', bk)])
                    stt('dve', xt[b][:, oc, :], ps[:, bk, :], G1[:, oc:oc + 1], xt[b][:, oc, :], ALU.mult, ALU.add,
                        [('ps', bk), ('xt', 0), 'modT'], [('xt', 0)])
                dma('sp', xo[:, :, tsl], xt[b], [('xt', 0)], [('x', T)], 'xs0')
            P.barrier()
            xsrc = xS
            if stop_after == 'merge':
                break

            phase_norm(xS, A2, B2)
            for pas in range(2):
                rx_reset()
                Wua = wv(0, [8, 1408]); Wub = wv(11264, [8, 1408]); Wd = wv(22528, [11, 1024])
                wload(Wua, Wl['up'][:, :, pas * 1408:(pas + 1) * 1408], key=('Wu', 0))
                wload(Wub, Wl['up'][:, :, DFF + pas * 1408: DFF + (pas + 1) * 1408], key=('Wu', 1))
                wload(Wd, Wl['down'][pas * 1408:(pas + 1) * 1408, :].rearrange("(j p) n -> p j n", p=128), key=('Wu', 2))
                WB = 516
                Ua = [xf([WB]), xf([WB])]; Ub = [xf([WB]), xf([WB])]
                ca = [xf([520]), xf([520])]; cbb = [xf([520]), xf([520])]
                Ha = xf([11, 2]); Hb = xf([11, 2])
                G2b = [xb([11, 520]), xb([11, 520])]
                xw1 = xf([8, 520])
                memset('dve', Ha, 0.0, ['Ha']); memset('dve', Hb, 0.0, ['Hb'])
                for q in range(2):
                    memset('pool', Ua[q][:, 514:516], 0.0, [('Ua', q)])
                    memset('pool', Ub[q][:, 514:516], 0.0, [('Ub', q)])
                xo = fm(xS)

                def geom(T):
                    c0 = 2 if T == 0 else 1
                    c1 = 514 if T == NT - 1 else 513
                    return c0, c1, c1 - c0, T * 512 + c0 - 2

                def emit_up(T, jjs, pas=pas, Wua=Wua, Wub=Wub, Ua=Ua, Ub=Ub, Ha=Ha, Hb=Hb, ca=ca, cbb=cbb, G2b=G2b):
                    tsl = slice(T * 512, (T + 1) * 512)
                    c0, c1, wd, tok0 = geom(T)
                    G = G2b[T % 2]; gk = ('G', T % 2)
                    for jj in jjs:
                        q = jj % 2
                        ja = pas * 11 + jj; jb = 22 + ja
                        bka = P.nextbank()
                        mmgroup(ps[:, bka, :], [(Wua[:, m, jj * 128:(jj + 1) * 128], hT[:, m, tsl]) for m in range(8)],
                                [('hT', T), ('Wu', 0)], [('ps', bka)])
                        bkb = P.nextbank()
                        mmgroup(ps[:, bkb, :], [(Wub[:, m, jj * 128:(jj + 1) * 128], hT[:, m, tsl]) for m in range(8)],
                                [('hT', T), ('Wu', 1)], [('ps', bkb)])
                        for (U, H, bk, jx, cc, nm, e2) in ((Ua, Ha, bka, ja, ca, 'a', 'dve'), (Ub, Hb, bkb, jb, cbb, 'b', 'pool')):
                            uk = ('U' + nm, q); hk = 'H' + nm; ck = ('c' + nm, q)
                            copy('act', U[q][:, 2:514], ps[:, bk, :], [('ps', bk)], [uk])
                            copy('pool', U[q][:, 0:2], H[:, jj, :], [hk], [uk])
                            copy('pool', H[:, jj, :], U[q][:, 512:514], [uk], [hk])
                            act(cc[q][:, 0:wd], U[q][:, c0:c1], AF.Identity, [uk, 'small'], [ck],
                                bias=cb_v[:, li, jx:jx + 1], scale=cw_v[:, li, 1, jx:jx + 1])
                            stt('dve', cc[q][:, 0:wd], U[q][:, c0 - 1:c1 - 1], cw_v[:, li, 0, jx:jx + 1], cc[q][:, 0:wd],
                                ALU.mult, ALU.add, [uk, ck, 'small'], [ck])
                            stt('dve', cc[q][:, 0:wd], U[q][:, c0 + 1:c1 + 1], cw_v[:, li, 2, jx:jx + 1], cc[q][:, 0:wd],
                                ALU.mult, ALU.add, [uk, ck, 'small'], [ck])
                        act(ca[q][:, 0:wd], ca[q][:, 0:wd], AF.Gelu, [('ca', q)], [('ca', q)])
                        tt('dve', G[:, jj, 0:wd], ca[q][:, 0:wd], cbb[q][:, 0:wd], ALU.mult, [('ca', q), ('cb', q)], [gk])

                def emit_down(T, Wd=Wd, G2b=G2b, xw1=xw1):
                    c0, c1, wd, tok0 = geom(T)
                    G = G2b[T % 2]; gk = ('G', T % 2)
                    dma('sp', xw1[:, :, 0:wd], xo[:, :, tok0:tok0 + wd], [('xwin', T)], ['xw'], 'xw0')
                    for oc in range(8):
                        for (s0, s1_) in ((0, min(wd, 512)),) + (((512, wd),) if wd > 512 else ()):
                            bk = P.nextbank()
                            mmgroup(ps[:, bk, 0:s1_ - s0],
                                    [(Wd[:, jj, oc * 128:(oc + 1) * 128], G[:, jj, s0:s1_]) for jj in range(11)],
                                    [gk, ('Wu', 2)], [('ps', bk)])
                            stt('dve', xw1[:, oc, s0:s1_], ps[:, bk, 0:s1_ - s0], G2[:, oc:oc + 1], xw1[:, oc, s0:s1_],
                                ALU.mult, ALU.add, [('ps', bk), 'xw', 'modT'], ['xw'])
                    dma('sp', xo[:, :, tok0:tok0 + wd], xw1[:, :, 0:wd], ['xw'], [('xwin', T)], 'xws0')

                for T in range(NT):
                    emit_up(T, range(0, 4))
                    if T > 0:
                        emit_down(T - 1)
                    emit_up(T, range(4, 11))
                emit_down(NT - 1)
                P.barrier()

        if final and stop_after is None:
            rx_reset()
            xt = [xf([8, 512]), xf([8, 512])]
            xsq = xb([8, 512])
            rs = [xf([512]), xf([512])]
            xv = fm(xsrc); yv = fm(yout)
            for t in range(NT):
                b = t % 2
                dma('sp', xt[b], xv[:, :, t * 512:(t + 1) * 512], [('x', t)], [('xt', b)], f'xt{b}')
                for k in range(8):
                    act(xsq[:, k, :], xt[b][:, k, :], AF.Square, [('xt', b)], [('xsq', k)])
                bk = P.nextbank()
                mmgroup(ps[:, bk, :], [(ones_b, xsq[:, k, :]) for k in range(8)], [('xsq', k) for k in range(8)], [('ps', bk)])
                act(rs[b], ps[:, bk, :], AF.Sqrt, [('ps', bk)], [('rs', b)], bias=eps_t, scale=1.0 / D)
                P.op('dve', lambda v, o=rs[b]: v.reciprocal(out=o, in_=o), [('rs', b)], [('rs', b)])
                for k in range(8):
                    stt('dve', xt[b][:, k, :], xt[b][:, k, :], fing_t[:, k:k + 1], rs[b], ALU.mult, ALU.mult,
                        [('xt', b), ('rs', b), 'small', ('xsq', k)], [('xt', b)])
                dma('sp', yv[:, :, t * 512:(t + 1) * 512], xt[b], [('xt', b)], [('y', t)], f'ys{b}')
        else:
            rx_reset()
            xt = [xf([8, 512]), xf([8, 512])]
            xv = fm(xsrc); yv = fm(yout)
            for t in range(NT):
                b = t % 2
                dma('sp', xt[b], xv[:, :, t * 512:(t + 1) * 512], [('x', t)], [('xt', b)], f'xt{b}')
                dma('sp', yv[:, :, t * 512:(t + 1) * 512], xt[b], [('xt', b)], [('y', t)], f'ys{b}')
        P.barrier()

        import os as _os
        _xpe = int(_os.environ.get('XTRA_PE', '0')); _xdve = int(_os.environ.get('XTRA_DVE', '0'))
        if _xpe:
            def fn(t):
                for i in range(_xpe):
                    ins = t.matmul(ps[:, 0, :], lhsT=ident_b, rhs=RX[:, 0:512], start=True, stop=True)
                return ins
            P.op('pe', fn, (), [('ps', 0)])
        if _xdve:
            def fn(v):
                for i in range(_xdve):
                    ins = v.memset(stat[:, 0:1], 0.0)
                return ins
            P.op('dve', fn, (), ['junkx'])
        _xdma = int(_os.environ.get('XTRA_DMA', '0'))
        if _dmy is not None:
            dma('sp', stat[:, 0:8], _dmy[:, 219000:219008], (), ['junkd'], 'junkd')
        if _xdma:
            rx_reset()
            xtt = xf([8, 512])
            for i in range(_xdma):
                dma('sp', xtt, fm(xin)[:, :, (i % 8) * 512:(i % 8 + 1) * 512], (), ['xtt'], 'xtt')
            P.barrier()
        P.finalize(nc, es)
        block = es.enter_context(nc.Block())

        @block.tensor
        def _(t):
            P.emit(t, 'pe')

        @block.scalar
        def _(a):
            P.emit(a, 'act')

        @block.vector
        def _(v):
            P.emit(v, 'dve')

        @block.gpsimd
        def _(g):
            P.emit(g, 'pool')

        @block.sync
        def _(s):
            P.emit(s, 'sp')
    return nc


_CACHE = {}


def _get_nc(key, *args, **kw):
    if key not in _CACHE:
        _CACHE[key] = build(*args, **kw)
    return _CACHE[key]


def _layer_inputs(inp, layers):
    f = lambda a: np.ascontiguousarray(np.asarray(a, dtype=np.float32))
    g = lambda k: np.asarray(inp[k], dtype=np.float32)
    d = {}
    adab = np.stack([g('ada_b')[l].reshape(48, 128).T for l in layers], axis=1).reshape(128, -1)
    n1g = np.stack([g('norm1_g')[l].reshape(8, 128).T for l in layers], axis=1).reshape(128, -1)
    n2g = np.stack([g('norm2_g')[l].reshape(8, 128).T for l in layers], axis=1).reshape(128, -1)
    cw = np.stack([g('conv_w')[l].reshape(3, 44, 128).transpose(2, 0, 1) for l in layers], axis=1).reshape(128, -1)
    cb = np.stack([g('conv_b')[l].reshape(44, 128).T for l in layers], axis=1).reshape(128, -1)
    fing = g('final_g').reshape(8, 128).T
    d['smalls'] = f(np.concatenate([adab, n1g, n2g, cw, cb, fing], axis=1))
    for i, l in enumerate(layers):
        d[f'wp{i}'] = f(np.concatenate([g('w_in')[l], g('w_fourier')[l], g('w_ret')[l], g('w_out')[l],
                                        g('ffn_up')[l], g('ada_w')[l], g('w_in')[l][:, :1024].T], axis=1))
    d['downs'] = f(np.stack([g('ffn_down')[l] for l in layers], axis=0))
    C = make_consts()
    d['c_128'] = f(np.stack([C['ident'], C['cg'], C['sg'], C['g3c'], C['g3s']], axis=1))
    d['c_tab'] = f(np.concatenate([C['dmaskT'], C['tabf'], C['tabb'], C['decf'], C['decb']], axis=1))
    d['c_rot'] = f(np.stack([C['cosT'], C['sinT']], axis=1))
    d['c_tmat'] = f(C['tmat'])
    return d


FUSED = True


def kernel(**inp):
    x = np.asarray(inp['x'], dtype=np.float32)
    c = np.asarray(inp['c'], dtype=np.float32)
    B = x.shape[0]
    xT = [np.ascontiguousarray(x[b].T) for b in range(B)]
    cvs = [np.ascontiguousarray(c[b].reshape(8, 128).T) for b in range(B)]
    if FUSED:
        nc = _get_nc('fused', [0, 1, 2, 3], True)
        shared = _layer_inputs(inp, [0, 1, 2, 3])
        maps = [dict(shared, xT=xT[b], cvec=cvs[b]) for b in range(B)]
        res = run_bass_kernel_spmd(nc, maps, core_ids=list(range(B)))
        outs = [res.results[b]['yT'] for b in range(B)]
    else:
        cur = xT
        for l in range(4):
            last = (l == 3)
            nc = _get_nc('last' if last else 'layer', [0], last)
            shared = _layer_inputs(inp, [l])
            maps = [dict(shared, xT=cur[b], cvec=cvs[b]) for b in range(B)]
            res = run_bass_kernel_spmd(nc, maps, core_ids=list(range(B)))
            cur = [res.results[b]['yT'] for b in range(B)]
        outs = cur
    return np.stack([np.ascontiguousarray(o.T) for o in outs], axis=0).astype(np.float32)
```

```python
import numpy as np
from contextlib import ExitStack
import concourse.bass as bass
import concourse.mybir as mybir
from concourse.bass_utils import run_bass_kernel_spmd

F32 = mybir.dt.float32
BF16 = mybir.dt.bfloat16
AF = mybir.ActivationFunctionType
ALU = mybir.AluOpType

S = 4096
D = 1024
NT = 8
DFF = 2816
EPS = 1e-6
ENGS = ['pe', 'act', 'dve', 'pool', 'sp']


class Op:
    __slots__ = ('eng', 'fn', 'dma', 'seq', 'signal', 'waits', 'sem', 'sigval', 'dmaval')


class Prog:
    def __init__(self):
        self.eng_ops = {e: [] for e in ENGS}
        self.last_w = {}
        self.readers = {}
        self.known = {e: {} for e in ENGS}
        self.dma_cnt = {}
        self.pending_dma = []
        self.last_compute = {e: None for e in ENGS}
        self.bank = 0
        self.bank_mod = 8

    def nextbank(self, n=1):
        if n == 2 and self.bank % 2:
            self.bank += 1
        b = self.bank % self.bank_mod
        self.bank += n
        return b

    def op(self, eng, fn, reads=(), writes=(), dma=None):
        o = Op()
        o.eng = eng; o.fn = fn; o.dma = dma; o.signal = False; o.waits = []
        o.sem = None; o.sigval = 0; o.dmaval = 0
        o.seq = len(self.eng_ops[eng])
        if dma is not None:
            self.dma_cnt[dma] = self.dma_cnt.get(dma, 0) + 1
            o.dmaval = 16 * self.dma_cnt[dma]
            assert o.dmaval < 60000, dma
        raw = set(); deps = []
        for k in reads:
            w = self.last_w.get(k)
            if w is not None:
                raw.add(id(w)); deps.append(w)
        for k in writes:
            w = self.last_w.get(k)
            if w is not None:
                deps.append(w)
            rd = self.readers.get(k)
            if rd:
                deps.extend(rd.values())
        kn = self.known[eng]
        for d in deps:
            if d is o:
                continue
            if d.dma is not None:
                key = ('d', d.dma)
                if kn.get(key, 0) < d.dmaval:
                    kn[key] = d.dmaval; o.waits.append(d)
            else:
                if d.eng == eng and dma is None and id(d) not in raw:
                    continue
                key = ('e', d.eng)
                if kn.get(key, -1) < d.seq:
                    kn[key] = d.seq; d.signal = True; o.waits.append(d)
        for k in writes:
            self.last_w[k] = o; self.readers[k] = {}
        for k in reads:
            r = self.readers.setdefault(k, {})
            r[eng if dma is None else ('d', id(o))] = o
        self.eng_ops[eng].append(o)
        if dma is not None:
            self.pending_dma.append(o)
        else:
            self.last_compute[eng] = o
        return o

    def barrier(self):
        lasts = dict(self.last_compute)
        pend = self.pending_dma; self.pending_dma = []
        for e in ENGS:
            o = Op()
            o.eng = e; o.fn = None; o.dma = None; o.signal = False; o.waits = []
            o.sem = None; o.sigval = 0; o.dmaval = 0
            o.seq = len(self.eng_ops[e])
            kn = self.known[e]
            for e2, d in lasts.items():
                if d is None or e2 == e:
                    continue
                key = ('e', e2)
                if kn.get(key, -1) < d.seq:
                    kn[key] = d.seq; d.signal = True; o.waits.append(d)
            for d in pend:
                key = ('d', d.dma)
                if kn.get(key, 0) < d.dmaval:
                    kn[key] = d.dmaval; o.waits.append(d)
            self.eng_ops[e].append(o)
        self.last_w = {}; self.readers = {}

    def finalize(self, nc, es):
        n = 0
        for e in ENGS:
            cnt = 0; cur = None
            for o in self.eng_ops[e]:
                if o.dma is None and o.fn is not None and o.signal:
                    if cur is None or cnt >= 30000:
                        cur = es.enter_context(nc.semaphore(f"s_{e}_{n}")); n += 1; cnt = 0
                    cnt += 1; o.sem = cur; o.sigval = cnt
        self.dsems = {name: es.enter_context(nc.semaphore("d_" + name)) for name in self.dma_cnt}

    def emit(self, h, e):
        for o in self.eng_ops[e]:
            for d in o.waits:
                if d.dma is not None:
                    h.wait_ge(self.dsems[d.dma], d.dmaval)
                else:
                    h.wait_ge(d.sem, d.sigval)
            if o.fn is not None:
                ins = o.fn(h)
                if o.dma is not None:
                    ins.then_inc(self.dsems[o.dma], 16)
                elif o.signal:
                    ins.then_inc(o.sem, 1)


def make_consts():
    c = {}
    j = np.arange(128)
    ang = 2.0 * np.pi * ((j[:, None] * j[None, :]) % 128) / 128.0
    c['cg'] = (np.cos(ang) / np.sqrt(128.0)).astype(np.float32)
    c['sg'] = (np.sin(ang) / np.sqrt(128.0)).astype(np.float32)
    c['ident'] = np.eye(128, dtype=np.float32)
    n1 = np.arange(128); k1 = np.arange(128)
    T = np.zeros((32, 128, 3, 128), np.float32)
    for n2 in range(32):
        n = 32 * n1 + n2
        a = 2.0 * np.pi * ((n[:, None] * k1[None, :]) % 4096) / 4096.0
        T[n2, :, 0, :] = np.cos(a); T[n2, :, 1, :] = np.sin(a); T[n2, :, 2, :] = -np.sin(a)
    c['tmat'] = T
    g3c = np.zeros((128, 128), np.float32); g3s = np.zeros((128, 128), np.float32)
    for n2 in range(32):
        for k2 in range(32):
            a = 2.0 * np.pi * ((n2 * k2) % 32) / 32.0
            for b in range(4):
                g3c[n2 * 4 + b, k2 * 4 + b] = np.cos(a) / 64.0
                g3s[n2 * 4 + b, k2 * 4 + b] = -np.sin(a) / 64.0
    c['g3c'] = g3c; c['g3s'] = g3s
    half = 128
    inv_freq = (10000.0 ** (-np.arange(half, dtype=np.float32) / half)).astype(np.float32)
    angr = (np.arange(S, dtype=np.float32)[:, None] * inv_freq[None, :]).astype(np.float32)
    c['cosT'] = np.ascontiguousarray(np.cos(angr).T.astype(np.float32))
    c['sinT'] = np.ascontiguousarray(np.sin(angr).T.astype(np.float32))
    hh = np.arange(4, dtype=np.float64)
    lgf = np.log1p(-np.exp2(-(5.0 + 0.0) - hh))
    lgb = np.log1p(-np.exp2(-(5.0 + 0.5) - hh))
    cc = np.arange(128, dtype=np.float64)
    diff = cc[:, None] - cc[None, :]
    dm = np.zeros((4, 128, 128))
    for h in range(4):
        dm[h] = np.where(diff >= 0, np.exp(lgf[h] * np.maximum(diff, 0)), np.exp(lgb[h] * np.maximum(-diff, 0)))
    c['dmaskT'] = np.ascontiguousarray(np.transpose(dm, (2, 0, 1)) / 16.0).astype(np.float32).reshape(128, 512)
    tabf = np.zeros((128, 8, 128)); tabb = np.zeros((128, 8, 128))
    for ch in range(8):
        h = ch // 2
        tabf[:, ch, :] = (np.exp(lgf[h] * (cc + 1.0)) / 16.0)[None, :]
        tabb[:, ch, :] = (np.exp(lgb[h] * (128.0 - cc)) / 16.0)[None, :]
    c['tabf'] = tabf.astype(np.float32).reshape(128, 1024)
    c['tabb'] = tabb.astype(np.float32).reshape(128, 1024)
    decf = np.zeros((128, 1024)); decb = np.zeros((128, 1024))
    for h in range(4):
        decf[:, h * 256:(h + 1) * 256] = np.exp(lgf[h] * (127.0 - cc))[:, None]
        decb[:, h * 256:(h + 1) * 256] = np.exp(lgb[h] * cc)[:, None]
    c['decf'] = decf.astype(np.float32); c['decb'] = decb.astype(np.float32)
    c['_gcf'] = [float(np.exp(lgf[h] * 128.0)) for h in range(4)]
    c['_gcb'] = [float(np.exp(lgb[h] * 128.0)) for h in range(4)]
    return c


CONST_SHAPES = {'cg': [128, 128], 'sg': [128, 128], 'ident': [128, 128], 'tmat': [32, 128, 3, 128],
                'g3c': [128, 128], 'g3s': [128, 128], 'cosT': [128, S], 'sinT': [128, S],
                'dmaskT': [128, 512], 'tabf': [128, 1024], 'tabb': [128, 1024],
                'decf': [128, 1024], 'decb': [128, 1024]}


def build(layers, final, stop_after=None, dbg=()):
    nc = bass.Bass("TRN2", target_bir_lowering=False)
    C = make_consts()
    gcf = C['_gcf']; gcb = C['_gcb']
    NL = len(layers)

    def din(name, shape):
        return nc.dram_tensor(name, shape, F32, kind="ExternalInput").ap()

    xin = din("xT", [D, S])
    cv = din("cvec", [128, 8])
    SMW = NL * 48 + NL * 8 + NL * 8 + NL * 132 + NL * 44 + 8
    smalls = din("smalls", [128, SMW])
    WPC = 7168 + 3 * D + 2 * DFF + 6 * D + D
    wps = [din(f"wp{li}", [D, WPC]) for li in range(NL)]
    downs = din("downs", [NL, DFF, D])
    W = {}
    for li in range(NL):
        wpv = wps[li].rearrange("(k p) n -> p k n", p=128)
        o = [0]

        def cut(n, wpv=wpv, o=o):
            a_ = wpv[:, :, o[0]:o[0] + n]; o[0] += n; return a_
        W[li] = dict(w_in=cut(7168), w_f=cut(D), w_ret=cut(D), w_out=cut(D), up=cut(2 * DFF), ada=cut(6 * D),
                     winfT=cut(D), down=downs[li])
    c128 = din("c_128", [128, 5, 128])
    ctab = din("c_tab", [128, 512 + 4 * 1024])
    crot = din("c_rot", [128, 2, S])
    K = {'tmat': din("c_tmat", [32, 128, 3, 128]),
         'ident': c128[:, 0, :], 'cg': c128[:, 1, :], 'sg': c128[:, 2, :], 'g3c': c128[:, 3, :], 'g3s': c128[:, 4, :],
         'dmaskT': ctab[:, 0:512], 'tabf': ctab[:, 512:1536], 'tabb': ctab[:, 1536:2560],
         'decf': ctab[:, 2560:3584], 'decb': ctab[:, 3584:4608],
         'cosT': crot[:, 0, :], 'sinT': crot[:, 1, :]}
    _dmy = None
    yout = nc.dram_tensor("yT", [D, S], F32, kind="ExternalOutput").ap()

    def dscr(name, shape, dt):
        kind = "ExternalOutput" if name in dbg else "Internal"
        return nc.dram_tensor(name, shape, dt, kind=kind).ap()

    xS = dscr("xS", [D, S], F32)
    Bd = dscr("Bd", [2, 32, 128, 512], BF16)
    ZfT = dscr("ZfT", [D, S], BF16)
    yrTd = dscr("yrTd", [D, S], BF16)
    Sbd = dscr("Sbd", [32, 128, 2048], BF16)
    kTd = dscr("kTd", [16, 128, 2048], BF16)
    vd = dscr("vd", [32, 128, 1024], BF16)
    hTd = dscr("hTd", [D, S], BF16) if "hTd" in dbg else None

    def fm(ap):
        return ap.rearrange("(k p) n -> p k n", p=128)

    P = Prog()
    es = ExitStack()
    with es:
        RH = es.enter_context(nc.sbuf_tensor("RH", [128, 32768], BF16))
        RW = es.enter_context(nc.sbuf_tensor("RW", [128, 33792], BF16))
        RX = es.enter_context(nc.sbuf_tensor("RX", [128, 32768], BF16))
        RXf = RX.bitcast(F32)
        ps = es.enter_context(nc.psum_tensor("ps", [128, 8, 512], F32))
        psb = ps.bitcast(BF16)
        smallf = es.enter_context(nc.sbuf_tensor("smallf", [128, 1536], F32))
        smallb = es.enter_context(nc.sbuf_tensor("smallb", [128, 2048], BF16))
        hT = RH[:, :].rearrange("p (k n) -> p k n", k=8)

        so = [0]

        def sf(n):
            a = smallf[:, so[0]:so[0] + n]; so[0] += n; return a
        cvt = sf(8); adab_t = sf(NL * 48); n1g_t = sf(NL * 8); n2g_t = sf(NL * 8)
        cw_t = sf(NL * 132); cb_t = sf(NL * 44); fing_t = sf(8)
        modT = sf(48); prm = sf(16); eps_t = sf(1); stat = sf(32)
        assert so[0] <= 1536
        adab_v = adab_t.rearrange("p (l j) -> p l j", l=NL)
        n1g_v = n1g_t.rearrange("p (l j) -> p l j", l=NL)
        n2g_v = n2g_t.rearrange("p (l j) -> p l j", l=NL)
        cw_v = cw_t.rearrange("p (l i j) -> p l i j", l=NL, i=3)
        cb_v = cb_t.rearrange("p (l j) -> p l j", l=NL)
        bo = [0]

        def sb(n):
            a = smallb[:, bo[0]:bo[0] + n]; bo[0] += n; return a
        cact = sb(8); ones_b = sb(128); ident_b = sb(128); cg_b = sb(128); sg_b = sb(128)
        g3c_b = sb(128); g3s_b = sb(128)
        assert bo[0] <= 2048

        rx = [0]

        def rx_reset():
            rx[0] = 0

        def xb(shape):
            n = int(np.prod(shape)); o = rx[0] // 2; rx[0] += n * 2
            assert rx[0] <= 65536, rx[0]
            a = RX[:, o:o + n]
            if len(shape) == 2:
                a = a.rearrange("p (a b) -> p a b", a=shape[0])
            elif len(shape) == 3:
                a = a.rearrange("p (a b c) -> p a b c", a=shape[0], b=shape[1])
            return a

        def xf(shape):
            rx[0] = (rx[0] + 3) // 4 * 4
            n = int(np.prod(shape)); o = rx[0] // 4; rx[0] += n * 4
            assert rx[0] <= 65536, rx[0]
            a = RXf[:, o:o + n]
            if len(shape) == 2:
                a = a.rearrange("p (a b) -> p a b", a=shape[0])
            elif len(shape) == 3:
                a = a.rearrange("p (a b c) -> p a b c", a=shape[0], b=shape[1])
            return a

        def wv(off, shape):
            n = int(np.prod(shape))
            assert off + n <= 33792
            a = RW[:, off:off + n]
            if len(shape) == 2:
                a = a.rearrange("p (a b) -> p a b", a=shape[0])
            elif len(shape) == 3:
                a = a.rearrange("p (a b c) -> p a b c", a=shape[0], b=shape[1])
            return a

        def dma(eng, out, in_, reads, writes, name):
            P.op(eng, lambda h, out=out, in_=in_: h.dma_start(out=out, in_=in_), reads, writes, dma=name)

        def mmgroup(out, pairs, reads, writes):
            def fn(t, out=out, pairs=pairs):
                n = len(pairs)
                for i, (l, r) in enumerate(pairs):
                    ins = t.matmul(out, lhsT=l, rhs=r, start=(i == 0), stop=(i == n - 1))
                return ins
            P.op('pe', fn, reads, writes)

        def act(out, in_, func, reads, writes, bias=None, scale=None, accum=None):
            def fn(a, out=out, in_=in_, func=func, bias=bias, scale=scale, accum=accum):
                kw = {}
                if bias is not None: kw['bias'] = bias
                if scale is not None: kw['scale'] = scale
                if accum is not None: kw['accum_out'] = accum
                return a.activation(out=out, in_=in_, func=func, **kw)
            P.op('act', fn, reads, writes)

        def tt(eng, out, in0, in1, op, reads, writes):
            P.op(eng, lambda v, out=out, in0=in0, in1=in1, op=op: v.tensor_tensor(out=out, in0=in0, in1=in1, op=op),
                 reads, writes)

        def stt(eng, out, in0, scalar, in1, op0, op1, reads, writes):
            P.op(eng, lambda v, out=out, in0=in0, scalar=scalar, in1=in1, op0=op0, op1=op1:
                 v.scalar_tensor_tensor(out=out, in0=in0, scalar=scalar, in1=in1, op0=op0, op1=op1), reads, writes)

        def ts(eng, out, in0, s1, s2, op0, op1, reads, writes):
            def fn(v, out=out, in0=in0, s1=s1, s2=s2, op0=op0, op1=op1):
                if s2 is None:
                    return v.tensor_scalar(out=out, in0=in0, scalar1=s1, scalar2=None, op0=op0)
                return v.tensor_scalar(out=out, in0=in0, scalar1=s1, scalar2=s2, op0=op0, op1=op1)
            P.op(eng, fn, reads, writes)

        def copy(eng, out, in_, reads, writes):
            if eng == 'act':
                P.op('act', lambda a, out=out, in_=in_: a.copy(out=out, in_=in_), reads, writes)
            else:
                P.op(eng, lambda v, out=out, in_=in_: v.tensor_copy(out=out, in_=in_), reads, writes)

        def memset(eng, ap, val, writes):
            P.op(eng, lambda v, ap=ap, val=val: v.memset(ap, val), (), writes)

        def wload(dst, src, name='RW', key='RW'):
            dma('pool', dst, src, (), [key], name)

        TK = [('hT', t) for t in range(NT)]

        dma('sp', cvt, cv, (), ['small'], 'small')
        dma('sp', smallf[:, 8:8 + SMW], smalls, (), ['small'], 'small')
        for nm, dst in (('ident', ident_b), ('cg', cg_b), ('sg', sg_b), ('g3c', g3c_b), ('g3s', g3s_b)):
            dma('pool', dst, K[nm], (), ['smallb'], 'smallb')
        memset('dve', ones_b, 1.0, ['smallb'])
        memset('dve', eps_t, EPS, ['small'])
        act(cact, cvt, AF.Silu, ['small'], ['cact'])
        P.barrier()

        xsrc = xin
        for li in range(NL):
            Wl = W[li]
            for half in range(2):
                Wa = wv(0, [8, 3072])
                wload(Wa, Wl['ada'][:, :, half * 3072:(half + 1) * 3072])

                def fn(t, Wa=Wa, half=half):
                    for jj in range(24):
                        j = half * 24 + jj
                        for k in range(8):
                            ins = t.matmul(ps[:, 0, j:j + 1], lhsT=Wa[:, k, jj * 128:(jj + 1) * 128],
                                           rhs=cact[:, k:k + 1], start=(k == 0), stop=(k == 7))
                    return ins
                P.op('pe', fn, ['RW', 'cact'], [('ps', 0)])
            tt('dve', modT, ps[:, 0, 0:48], adab_v[:, li, :], ALU.add, [('ps', 0), 'small'], ['modT'])
            stt('dve', prm[:, 0:8], modT[:, 8:16], 1.0, n1g_v[:, li, :], ALU.add, ALU.mult, ['modT', 'small'], ['prm'])
            stt('dve', prm[:, 8:16], modT[:, 32:40], 1.0, n2g_v[:, li, :], ALU.add, ALU.mult, ['modT', 'small'], ['prm'])
            P.barrier()
            A1 = prm[:, 0:8]; A2 = prm[:, 8:16]
            B1 = modT[:, 0:8]; G1 = modT[:, 16:24]; B2 = modT[:, 24:32]; G2 = modT[:, 40:48]

            def phase_norm(xsrc, A, B):
                rx_reset()
                xt = [xf([8, 512]), xf([8, 512])]
                xsq = xb([8, 512])
                rs = [xf([512]), xf([512])]
                tmp = [xf([512]), xf([512])]
                xv = fm(xsrc)
                for t in range(NT):
                    b = t % 2
                    dma('sp', xt[b], xv[:, :, t * 512:(t + 1) * 512], [('x', t)], [('xt', b)], f'xt{b}')
                    for k in range(8):
                        act(xsq[:, k, :], xt[b][:, k, :], AF.Square, [('xt', b)], [('xsq', k)])
                    bk = P.nextbank()
                    mmgroup(ps[:, bk, :], [(ones_b, xsq[:, k, :]) for k in range(8)],
                            [('xsq', k) for k in range(8)], [('ps', bk)])
                    act(rs[b], ps[:, bk, :], AF.Sqrt, [('ps', bk)], [('rs', b)], bias=eps_t, scale=1.0 / D)
                    P.op('dve', lambda v, o=rs[b]: v.reciprocal(out=o, in_=o), [('rs', b)], [('rs', b)])
                    for k in range(8):
                        stt('dve', tmp[k % 2], xt[b][:, k, :], A[:, k:k + 1], rs[b], ALU.mult, ALU.mult,
                            [('xt', b), ('rs', b), 'prm'], [('tmp', k % 2)])
                        act(hT[:, k, t * 512:(t + 1) * 512], tmp[k % 2], AF.Identity,
                            [('tmp', k % 2), 'modT'], [('hT', t)], bias=B[:, k:k + 1], scale=1.0)
                P.barrier()

            phase_norm(xsrc, A1, B1)
            if hTd is not None:
                dma('sp', fm(hTd), hT, TK, ['hTd'], 'dbg')
                P.barrier()
            if stop_after == 'norm1':
                break

            rx_reset()
            Wf_b = xb([8, 1024]); WiT_b = xb([8, 1024])
            Wcs = wv(0, [8, 2048])
            M2 = wv(16384, [8, 2048])
            dma('pool', Wf_b, Wl['w_f'], (), ['Wf'], 'Wf')
            dma('pool', WiT_b, Wl['winfT'], (), ['WiT'], 'WiT')
            ev = 0
            for g in range(8):
                for cs in range(2):
                    for ot in range(2):
                        bk = P.nextbank()
                        mmgroup(ps[:, bk, :], [((cg_b if cs == 0 else sg_b), Wf_b[:, g, ot * 512:(ot + 1) * 512])],
                                ['Wf', 'smallb'], [('ps', bk)])
                        copy('act' if ev % 2 else 'dve', M2[:, g, cs * 1024 + ot * 512: cs * 1024 + (ot + 1) * 512],
                             ps[:, bk, :], [('ps', bk)], ['M2'])
                        ev += 1
            for m in range(8):
                for cs in range(2):
                    for ot in range(2):
                        bk = P.nextbank()
                        mmgroup(ps[:, bk, :],
                                [(WiT_b[:, c, m * 128:(m + 1) * 128], M2[:, c, cs * 1024 + ot * 512: cs * 1024 + (ot + 1) * 512])
                                 for c in range(8)], ['WiT', 'M2'], [('ps', bk)])
                        for q in range(2):
                            ct4 = 2 * ot + q
                            copy('act' if ev % 2 else 'dve', Wcs[:, m, ct4 * 512 + cs * 256: ct4 * 512 + (cs + 1) * 256],
                                 ps[:, bk, q * 256:(q + 1) * 256], [('ps', bk)], ['Wcs'])
                            ev += 1
            P.barrier()

            rx_reset()
            Dt = [xb([512]), xb([512])]
            Bt = [xb([512]), xb([512])]
            Tt = [xb([3, 128]), xb([3, 128])]
            Zt = xb([2, 4096])
            Zt4 = Zt.rearrange("p c (j k) -> p c j k", k=32)
            Bp = wv(16384, [32, 512])
            def f_stage01(ct4, n2):
                    b = n2 % 2
                    dma('pool', Tt[b], K['tmat'][n2], (), [('Tt', b)], f'Tt{b}')
                    bk = P.nextbank()
                    mmgroup(ps[:, bk, :], [(hT[:, m, n2::32], Wcs[:, m, ct4 * 512:(ct4 + 1) * 512]) for m in range(8)],
                            TK + ['Wcs'], [('ps', bk)])
                    copy('act', Dt[b], ps[:, bk, :], [('ps', bk)], [('Dt', b)])
                    bk2 = P.nextbank()

                    def fn(t, bk2=bk2, b=b):
                        t.matmul(ps[:, bk2, 0:256], lhsT=Tt[b][:, 0, :], rhs=Dt[b][:, 0:256], start=True, stop=False)
                        t.matmul(ps[:, bk2, 0:256], lhsT=Tt[b][:, 2, :], rhs=Dt[b][:, 256:512], start=False, stop=True)
                        t.matmul(ps[:, bk2, 256:512], lhsT=Tt[b][:, 0, :], rhs=Dt[b][:, 256:512], start=True, stop=False)
                        return t.matmul(ps[:, bk2, 256:512], lhsT=Tt[b][:, 1, :], rhs=Dt[b][:, 0:256], start=False, stop=True)
                    P.op('pe', fn, [('Tt', b), ('Dt', b)], [('ps', bk2)])
                    copy('dve', Bt[b], ps[:, bk2, :], [('ps', bk2)], [('Bt', b)])
                    dma('sp', Bd[ct4 % 2, n2], Bt[b], [('Bt', b)], [('Bd', ct4 % 2, n2 // 16)], f'Bts{b}')

            def f_load(ct4):
                Bdv = Bd[ct4 % 2].rearrange("n (a k) c -> (n a) k c", a=4)
                for hf in range(2):
                    dma('sp', Bp[:, hf * 16:(hf + 1) * 16, :], Bdv[:, hf * 16:(hf + 1) * 16, :],
                        [('Bd', ct4 % 2, 0), ('Bd', ct4 % 2, 1)], [('Bp', hf)], f'Bp{hf}')

            def f_stage3(ct4):
                for chh in range(2):
                    for k0 in range(0, 32, 4):
                        bk = P.nextbank()

                        def fn(t, bk=bk, k0=k0, chh=chh):
                            for q in range(4):
                                kk = k0 + q
                                t.matmul(ps[:, bk, q * 128:(q + 1) * 128], lhsT=Bp[:, kk, chh * 128:(chh + 1) * 128],
                                         rhs=g3c_b, start=True, stop=False)
                                ins = t.matmul(ps[:, bk, q * 128:(q + 1) * 128],
                                               lhsT=Bp[:, kk, 256 + chh * 128:256 + (chh + 1) * 128],
                                               rhs=g3s_b, start=False, stop=True)
                            return ins
                        P.op('pe', fn, [('Bp', k0 // 16), 'smallb'], [('ps', bk)])
                        copy('act' if (k0 // 4) % 2 else 'dve', Zt4[:, chh, :, k0:k0 + 4],
                             ps[:, bk, :].rearrange("p (q j) -> p j q", q=4), [('ps', bk)], ['Zt'])
                dma('sp', fm(ZfT)[:, ct4 * 2:ct4 * 2 + 2, :], Zt, ['Zt'], [('ZfT', t) for t in range(NT)], 'Zts')

            for ct4 in range(4):
                for n2 in range(32):
                    if n2 == 0 and ct4 > 0:
                        f_load(ct4 - 1)
                    if n2 == 8 and ct4 > 0:
                        f_stage3(ct4 - 1)
                    f_stage01(ct4, n2)
            f_load(3)
            f_stage3(3)
            P.barrier()
            if stop_after == 'fourier':
                break

            rx_reset()
            Wqkvg = wv(0, [8, 4096])
            for i in range(4):
                wload(Wqkvg[:, :, i * 1024:(i + 1) * 1024], Wl['w_in'][:, :, 1024 + i * 1024: 2048 + i * 1024],
                      key=('Wr', i))
            RT = 256; NRT = S // RT; CPT = RT // 128
            cst = [xf([RT]), xf([RT])]
            snt = [xf([RT]), xf([RT])]
            tf1 = xf([RT]); tf2 = xf([RT])
            pa = xf([RT]); pb_ = xf([RT]); pc = pa; pd = pb_
            kT = xb([8, RT]); qT = xb([8, RT])
            kd = xb([1024]); v2 = [xb([1024]), xb([1024])]
            Sst = xf([4, 512])
            Sbf = xb([4, 512])
            Sbl = [xb([4, 512])] * 2
            scT = xb([512]); sg2 = [xb([1024]), xb([1024])]; yn2 = [xb([1024]), xb([1024])]; yn = yn2[0]
            yrT = xb([8, RT])
            qdf = xb([8, 128]); qdb = xb([8, 128])
            tabf_b = xb([8, 128]); tabb_b = xb([8, 128]); decf_b = xb([1024]); decb_b = xb([1024]); dmask_b = xb([512])
            junk = None
            dma('pool', tabf_b.rearrange("p a b -> p (a b)"), K['tabf'], (), ['rc'], 'rc')
            dma('pool', tabb_b.rearrange("p a b -> p (a b)"), K['tabb'], (), ['rc'], 'rc')
            dma('pool', decf_b, K['decf'], (), ['rc'], 'rc')
            dma('pool', decb_b, K['decb'], (), ['rc'], 'rc')
            dma('pool', dmask_b, K['dmaskT'], (), ['rc'], 'rc')
            s1 = stat[:, 0:4]; s2 = stat[:, 4:8]; mean = stat[:, 8:12]; msq = stat[:, 12:16]
            var = stat[:, 16:20]; rstd = stat[:, 20:24]; nbias = stat[:, 24:28]

            def rotary_proj(T, col0, dst, dkey):
                b = T % 2
                for h in range(4):
                    bks = []
                    for q in range(2):
                        ch = 2 * h + q
                        bk = P.nextbank(); bks.append(bk)
                        mmgroup(ps[:, bk, 0:RT],
                                [(Wqkvg[:, m, col0 + ch * 128: col0 + (ch + 1) * 128], hT[:, m, T * RT:(T + 1) * RT])
                                 for m in range(8)], [('hT', (T * RT) // 512), ('Wr', col0 // 1024)], [('ps', bk)])
                    copy('act', tf1, ps[:, bks[0], 0:RT], [('ps', bks[0])], ['tf1'])
                    copy('act', tf2, ps[:, bks[1], 0:RT], [('ps', bks[1])], ['tf2'])
                    tt('dve', pa, tf1, cst[b], ALU.mult, ['tf1', ('cs', b)], ['pa'])
                    tt('dve', pb_, tf2, snt[b], ALU.mult, ['tf2', ('cs', b)], ['pb'])
                    tt('dve', dst[:, 2 * h, :], pa, pb_, ALU.subtract, ['pa', 'pb'], [dkey])
                    tt('dve', pc, tf1, snt[b], ALU.mult, ['tf1', ('cs', b), 'pa'], ['pa'])
                    tt('dve', pd, tf2, cst[b], ALU.mult, ['tf2', ('cs', b), 'pb'], ['pb'])
                    tt('dve', dst[:, 2 * h + 1, :], pc, pd, ALU.add, ['pa', 'pb'], [dkey])

            def load_cs(T):
                b = T % 2
                dma('sp', cst[b], K['cosT'][:, T * RT:(T + 1) * RT], (), [('cs', b)], f'cs{b}')
                dma('sp', snt[b], K['sinT'][:, T * RT:(T + 1) * RT], (), [('cs', b)], f'cs{b}')

            def kv_chunk(T, c4, dec_b, fwd):
                n = T * CPT + c4
                v_t = v2[n % 2]; vk = ('v_t', n % 2)
                if not fwd:
                    bk = P.nextbank(2)
                    for ot in range(2):
                        mmgroup(ps[:, bk + ot, :],
                                [(hT[:, m, n * 128:(n + 1) * 128], Wqkvg[:, m, 2048 + ot * 512: 2048 + (ot + 1) * 512])
                                 for m in range(8)], [('hT', n // 4), ('Wr', 2)], [('ps', bk + ot)])
                    copy('act', v_t.rearrange("p (a b) -> p a b", a=2), ps[:, bk:bk + 2, :],
                         [('ps', bk), ('ps', bk + 1)], [vk])
                    dma('act', vd[n], v_t, [vk], [('vd', n)], f'vts{n % 2}')
                else:
                    dma('sp', v_t, vd[n], [('vd', n)], [vk], f'vtl{n % 2}')
                bk = P.nextbank()

                def fn(t, bk=bk, c4=c4):
                    for j in range(8):
                        ins = t.transpose(psb[:, bk, j * 128:(j + 1) * 128], kT[:, j, c4 * 128:(c4 + 1) * 128], ident_b)
                    return ins
                P.op('pe', fn, ['kT', 'smallb'], [('ps', bk)])
                tt('dve', kd, psb[:, bk, :], dec_b, ALU.mult, [('ps', bk), 'rc'], ['kd'])
                return v_t, vk

            def state_update(gc, v_t, vk):
                for h in range(4):
                    bk = P.nextbank()

                    def fn(t, bk=bk, h=h, v_t=v_t):
                        for dc in range(2):
                            ins = t.matmul(ps[:, bk, dc * 256:(dc + 1) * 256],
                                           lhsT=kd[:, h * 256 + dc * 128: h * 256 + (dc + 1) * 128],
                                           rhs=v_t[:, h * 256:(h + 1) * 256], start=True, stop=True)
                        return ins
                    P.op('pe', fn, ['kd', vk], [('ps', bk)])
                    stt('dve', Sst[:, h, :], Sst[:, h, :], gc[h], ps[:, bk, :], ALU.mult, ALU.add,
                        [('ps', bk), 'Sst'], ['Sst'])

            memset('dve', Sst, 0.0, ['Sst'])
            for T in range(NRT - 1, -1, -1):
                load_cs(T)
                rotary_proj(T, 1024, kT, 'kT')
                dma('pool', kTd[T], kT.rearrange("p a b -> p (a b)"), ['kT'], [('kTd', T)], 'kTs')
                for c4 in range(CPT - 1, -1, -1):
                    n = T * CPT + c4
                    v_t, vk = kv_chunk(T, c4, decb_b, False)
                    copy('dve', Sbl[0], Sst, ['Sst'], [('Sbl', 0)])
                    dma('pool', Sbd[n], Sbl[0].rearrange("p a b -> p (a b)"), [('Sbl', 0)], [('Sbd', n)], 'Sbs0')
                    state_update(gcb, v_t, vk)
            memset('dve', Sst, 0.0, ['Sst'])
            memset('dve', Sbf, 0.0, ['Sbf'])
            P.bank_mod = 4
            ctx = {}

            def fA(T, c4):
                n = T * CPT + c4
                cs_ = slice(c4 * 128, (c4 + 1) * 128)
                if c4 == 0:
                    load_cs(T)
                    rotary_proj(T, 0, qT, 'qT')
                    dma('sp', kT.rearrange("p a b -> p (a b)"), kTd[T], [('kTd', T)], ['kT'], 'kTl')
                v_t, vk = kv_chunk(T, c4, decf_b, True)
                dma('sp', Sbl[0].rearrange("p a b -> p (a b)"), Sbd[n], [('Sbd', n)], [('Sbl', 0)], 'Sbl0')
                sg_t = sg2[n % 2]; sgk = ('sg', n % 2)
                bk = P.nextbank(2)
                for ot in range(2):
                    mmgroup(ps[:, bk + ot, :],
                            [(hT[:, m, n * 128:(n + 1) * 128], Wqkvg[:, m, 3072 + ot * 512: 3072 + (ot + 1) * 512])
                             for m in range(8)], [('hT', n // 4), ('Wr', 3)], [('ps', bk + ot)])
                act(sg_t.rearrange("p (a b) -> p a b", a=2), ps[:, bk:bk + 2, :], AF.Silu,
                    [('ps', bk), ('ps', bk + 1)], [sgk])
                bk = P.nextbank()

                def fn(t, bk=bk, cs_=cs_):
                    for h in range(4):
                        for dc in range(2):
                            ins = t.matmul(ps[:, bk, h * 128:(h + 1) * 128], lhsT=kT[:, 2 * h + dc, cs_],
                                           rhs=qT[:, 2 * h + dc, cs_], start=(dc == 0), stop=(dc == 1))
                    return ins
                P.op('pe', fn, ['kT', 'qT'], [('ps', bk)])
                tt('dve', scT, ps[:, bk, :], dmask_b, ALU.mult, [('ps', bk), 'rc'], ['scT'])
                tt('dve', qdf, qT[:, :, cs_], tabf_b, ALU.mult, ['qT', 'rc'], ['qdf'])
                tt('dve', qdb, qT[:, :, cs_], tabb_b, ALU.mult, ['qT', 'rc'], ['qdb'])
                bky = 4 + 2 * (n % 2)

                def fn(t, bky=bky, n=n, v_t=v_t):
                    for h in range(4):
                        o = ps[:, bky + h // 2, (h % 2) * 256:(h % 2 + 1) * 256]
                        t.matmul(o, lhsT=scT[:, h * 128:(h + 1) * 128], rhs=v_t[:, h * 256:(h + 1) * 256],
                                 start=True, stop=False)
                        for dc in range(2):
                            t.matmul(o, lhsT=qdf[:, 2 * h + dc, :], rhs=Sbf[:, h, dc * 256:(dc + 1) * 256],
                                     start=False, stop=False)
                        for dc in range(2):
                            ins = t.matmul(o, lhsT=qdb[:, 2 * h + dc, :], rhs=Sbl[0][:, h, dc * 256:(dc + 1) * 256],
                                           start=False, stop=(dc == 1))
                    return ins
                P.op('pe', fn, ['scT', vk, 'qdf', 'qdb', 'Sbf', ('Sbl', 0)], [('ps', bky), ('ps', bky + 1)])
                state_update(gcf, v_t, vk)
                copy('dve', Sbf, Sst, ['Sst'], ['Sbf'])
                ctx[n] = (bky, sg_t, sgk)

            def fB1(T, c4):
                n = T * CPT + c4
                bky, sg_t, sgk = ctx[n]
                yn_ = yn2[n % 2]; ynk = ('yn', n % 2); junk_ = yn_[:, 0:256]
                yk = [('ps', bky), ('ps', bky + 1)]
                for h in range(4):
                    o = ps[:, bky + h // 2, (h % 2) * 256:(h % 2 + 1) * 256]
                    act(junk_, o, AF.Copy, yk, [ynk, 'st1'], accum=s1[:, h:h + 1])
                    act(junk_, o, AF.Square, yk, [ynk, 'st1'], accum=s2[:, h:h + 1])
                act(msq, s1, AF.Square, ['st1'], ['msq'], scale=1.0 / 256.0)
                act(mean, msq, AF.Identity, ['msq'], ['mean'], bias=eps_t, scale=-1.0)
                for h in range(4):
                    act(var[:, h:h + 1], s2[:, h:h + 1], AF.Sqrt, ['st1', 'mean'], ['var'], bias=mean[:, h:h + 1],
                        scale=1.0 / 256.0)
                P.op('dve', lambda v: v.reciprocal(out=rstd, in_=var), ['var'], ['rstd'])
                stt('dve', nbias, s1, -1.0 / 256.0, rstd, ALU.mult, ALU.mult, ['st1', 'rstd'], ['nbias'])
                for h in range(4):
                    o = ps[:, bky + h // 2, (h % 2) * 256:(h % 2 + 1) * 256]
                    ts('dve', yn_[:, h * 256:(h + 1) * 256], o, rstd[:, h:h + 1], nbias[:, h:h + 1], ALU.mult, ALU.add,
                       yk + ['rstd', 'nbias'], [ynk])
                tt('dve', yn_, yn_, sg_t, ALU.mult, [ynk, sgk], [ynk])

            def fB2(T, c4):
                n = T * CPT + c4
                cs_ = slice(c4 * 128, (c4 + 1) * 128)
                ctx.pop(n)
                yn_ = yn2[n % 2]; ynk = ('yn', n % 2)
                bk = P.nextbank()

                def fn(t, bk=bk, yn_=yn_):
                    for j in range(8):
                        ins = t.transpose(psb[:, bk, j * 128:(j + 1) * 128], yn_[:, j * 128:(j + 1) * 128], ident_b)
                    return ins
                P.op('pe', fn, [ynk, 'smallb'], [('ps', bk)])
                copy('act', yrT[:, :, cs_], psb[:, bk, :].rearrange("p (a b) -> p a b", a=8), [('ps', bk)], ['yrT'])
                if c4 == CPT - 1:
                    dma('act', fm(yrTd)[:, :, T * RT:(T + 1) * RT], yrT, ['yrT'], [('yrTd', T)], 'yrs')

            seq = [(T, c4) for T in range(NRT) for c4 in range(CPT)]
            NS = len(seq)
            for i in range(NS + 2):
                if i < NS:
                    fA(*seq[i])
                if 1 <= i <= NS:
                    fB1(*seq[i - 1])
                if 2 <= i <= NS + 1:
                    fB2(*seq[i - 2])
            P.bank_mod = 8
            P.barrier()
            if stop_after == 'ret':
                break

            rx_reset()
            Wm = wv(0, [8, 4096])
            wload(Wm[:, :, 0:1024], Wl['w_in'][:, :, 5120:6144], key=('Wm', 0))
            wload(Wm[:, :, 1024:2048], Wl['w_in'][:, :, 6144:7168], key=('Wm', 1))
            wload(Wm[:, :, 2048:3072], Wl['w_ret'], key=('Wm', 2))
            wload(Wm[:, :, 3072:4096], Wl['w_out'], key=('Wm', 3))
            xt1 = xf([8, 512]); xt = [xt1, xt1]
            zf1 = xb([8, 512]); zf = [zf1, zf1]
            yr1 = xb([8, 512]); yr = [yr1, yr1]
            mg = xb([8, 512])
            saf = [xf([512]), xf([512])]; sar = [xf([512]), xf([512])]
            m1 = [xf([512]), xf([512])]; m2 = [xf([512]), xf([512])]
            xv = fm(xsrc); xo = fm(xS)
            for T in range(NT):
                b = T % 2
                tsl = slice(T * 512, (T + 1) * 512)
                dma('sp', xt[b], xv[:, :, tsl], [('x', T)], [('xt', 0)], 'xt0')
                dma('sp', zf[b], fm(ZfT)[:, :, tsl], [('ZfT', T)], [('zf', 0)], 'zf0')
                dma('sp', yr[b], fm(yrTd)[:, :, tsl], [('yrTd', 2 * T), ('yrTd', 2 * T + 1)], [('yr', 0)], 'yr0')
                for oc in range(8):
                    q = oc % 2
                    osl = slice(oc * 128, (oc + 1) * 128)
                    bk = P.nextbank()
                    mmgroup(ps[:, bk, :], [(Wm[:, m, osl], hT[:, m, tsl]) for m in range(8)],
                            [('hT', T), ('Wm', 0)], [('ps', bk)])
                    act(saf[q], ps[:, bk, :], AF.Sigmoid, [('ps', bk)], [('saf', q)])
                    bk = P.nextbank()
                    mmgroup(ps[:, bk, :], [(Wm[:, m, 1024 + oc * 128: 1024 + (oc + 1) * 128], hT[:, m, tsl]) for m in range(8)],
                            [('hT', T), ('Wm', 1)], [('ps', bk)])
                    act(sar[q], ps[:, bk, :], AF.Sigmoid, [('ps', bk)], [('sar', q)])
                    bk = P.nextbank()
                    mmgroup(ps[:, bk, :], [(Wm[:, m, 2048 + oc * 128: 2048 + (oc + 1) * 128], yr[b][:, m, :]) for m in range(8)],
                            [('yr', 0), ('Wm', 2)], [('ps', bk)])
                    tt('dve', m1[q], saf[q], zf[b][:, oc, :], ALU.mult, [('saf', q), ('zf', 0)], [('m1', q)])
                    tt('dve', m2[q], ps[:, bk, :], sar[q], ALU.mult, [('ps', bk), ('sar', q)], [('m2', q)])
                    tt('dve', mg[:, oc, :], m1[q], m2[q], ALU.add, [('m1', q), ('m2', q)], ['mg'])
                for oc in range(8):
                    bk = P.nextbank()
                    mmgroup(ps[:, bk, :], [(Wm[:, m, 3072 + oc * 128: 3072 + (oc + 1) * 128], mg[:, m, :]) for m in range(8)],
                            ['mg', ('Wm', 3)], [('ps', bk)])
                    stt('dve', xt[b][:, oc, :], ps[:, bk, :], G1[:, oc:oc + 1], xt[b][:, oc, :], ALU.mult, ALU.add,
                        [('ps', bk), ('xt', 0), 'modT'], [('xt', 0)])
                dma('pool', xo[:, :, tsl], xt[b], [('xt', 0)], [('x', T)], 'xs0')
            P.barrier()
            xsrc = xS
            if stop_after == 'merge':
                break

            phase_norm(xS, A2, B2)
            for pas in range(2):
                rx_reset()
                Wua = wv(0, [8, 1408]); Wub = wv(11264, [8, 1408]); Wd = wv(22528, [11, 1024])
                wload(Wua, Wl['up'][:, :, pas * 1408:(pas + 1) * 1408], key=('Wu', 0))
                wload(Wub, Wl['up'][:, :, DFF + pas * 1408: DFF + (pas + 1) * 1408], key=('Wu', 1))
                wload(Wd, Wl['down'][pas * 1408:(pas + 1) * 1408, :].rearrange("(j p) n -> p j n", p=128), key=('Wu', 2))
                WB = 516
                Ua = [xf([WB]), xf([WB])]; Ub = [xf([WB]), xf([WB])]
                ca = [xf([520]), xf([520])]; cbb = [xf([520]), xf([520])]
                Ha = xf([11, 2]); Hb = xf([11, 2])
                G2b = [xb([11, 520]), xb([11, 520])]
                xw1 = xf([8, 520])
                memset('dve', Ha, 0.0, ['Ha']); memset('dve', Hb, 0.0, ['Hb'])
                for q in range(2):
                    memset('pool', Ua[q][:, 514:516], 0.0, [('Ua', q)])
                    memset('pool', Ub[q][:, 514:516], 0.0, [('Ub', q)])
                xo = fm(xS)

                def geom(T):
                    c0 = 2 if T == 0 else 1
                    c1 = 514 if T == NT - 1 else 513
                    return c0, c1, c1 - c0, T * 512 + c0 - 2

                def emit_up(T, jjs, pas=pas, Wua=Wua, Wub=Wub, Ua=Ua, Ub=Ub, Ha=Ha, Hb=Hb, ca=ca, cbb=cbb, G2b=G2b):
                    tsl = slice(T * 512, (T + 1) * 512)
                    c0, c1, wd, tok0 = geom(T)
                    G = G2b[T % 2]; gk = ('G', T % 2)
                    for jj in jjs:
                        q = jj % 2
                        ja = pas * 11 + jj; jb = 22 + ja
                        bka = P.nextbank()
                        mmgroup(ps[:, bka, :], [(Wua[:, m, jj * 128:(jj + 1) * 128], hT[:, m, tsl]) for m in range(8)],
                                [('hT', T), ('Wu', 0)], [('ps', bka)])
                        bkb = P.nextbank()
                        mmgroup(ps[:, bkb, :], [(Wub[:, m, jj * 128:(jj + 1) * 128], hT[:, m, tsl]) for m in range(8)],
                                [('hT', T), ('Wu', 1)], [('ps', bkb)])
                        for (U, H, bk, jx, cc, nm, e2) in ((Ua, Ha, bka, ja, ca, 'a', 'dve'), (Ub, Hb, bkb, jb, cbb, 'b', 'pool')):
                            uk = ('U' + nm, q); hk = 'H' + nm; ck = ('c' + nm, q)
                            copy('act', U[q][:, 2:514], ps[:, bk, :], [('ps', bk)], [uk])
                            copy('pool', U[q][:, 0:2], H[:, jj, :], [hk], [uk])
                            copy('pool', H[:, jj, :], U[q][:, 512:514], [uk], [hk])
                            act(cc[q][:, 0:wd], U[q][:, c0:c1], AF.Identity, [uk, 'small'], [ck],
                                bias=cb_v[:, li, jx:jx + 1], scale=cw_v[:, li, 1, jx:jx + 1])
                            stt('dve', cc[q][:, 0:wd], U[q][:, c0 - 1:c1 - 1], cw_v[:, li, 0, jx:jx + 1], cc[q][:, 0:wd],
                                ALU.mult, ALU.add, [uk, ck, 'small'], [ck])
                            stt('dve', cc[q][:, 0:wd], U[q][:, c0 + 1:c1 + 1], cw_v[:, li, 2, jx:jx + 1], cc[q][:, 0:wd],
                                ALU.mult, ALU.add, [uk, ck, 'small'], [ck])
                        act(ca[q][:, 0:wd], ca[q][:, 0:wd], AF.Gelu, [('ca', q)], [('ca', q)])
                        tt('dve', G[:, jj, 0:wd], ca[q][:, 0:wd], cbb[q][:, 0:wd], ALU.mult, [('ca', q), ('cb', q)], [gk])

                def emit_down(T, Wd=Wd, G2b=G2b, xw1=xw1):
                    c0, c1, wd, tok0 = geom(T)
                    G = G2b[T % 2]; gk = ('G', T % 2)
                    dma('sp', xw1[:, :, 0:wd], xo[:, :, tok0:tok0 + wd], [('xwin', T)], ['xw'], 'xw0')
                    for oc in range(8):
                        for (s0, s1_) in ((0, min(wd, 512)),) + (((512, wd),) if wd > 512 else ()):
                            bk = P.nextbank()
                            mmgroup(ps[:, bk, 0:s1_ - s0],
                                    [(Wd[:, jj, oc * 128:(oc + 1) * 128], G[:, jj, s0:s1_]) for jj in range(11)],
                                    [gk, ('Wu', 2)], [('ps', bk)])
                            stt('dve', xw1[:, oc, s0:s1_], ps[:, bk, 0:s1_ - s0], G2[:, oc:oc + 1], xw1[:, oc, s0:s1_],
                                ALU.mult, ALU.add, [('ps', bk), 'xw', 'modT'], ['xw'])
                    dma('sp', xo[:, :, tok0:tok0 + wd], xw1[:, :, 0:wd], ['xw'], [('xwin', T)], 'xws0')

                for T in range(NT):
                    emit_up(T, range(0, 4))
                    if T > 0:
                        emit_down(T - 1)
                    emit_up(T, range(4, 11))
                emit_down(NT - 1)
                P.barrier()

        if final and stop_after is None:
            rx_reset()
            xt = [xf([8, 512]), xf([8, 512])]
            xsq = xb([8, 512])
            rs = [xf([512]), xf([512])]
            xv = fm(xsrc); yv = fm(yout)
            for t in range(NT):
                b = t % 2
                dma('sp', xt[b], xv[:, :, t * 512:(t + 1) * 512], [('x', t)], [('xt', b)], f'xt{b}')
                for k in range(8):
                    act(xsq[:, k, :], xt[b][:, k, :], AF.Square, [('xt', b)], [('xsq', k)])
                bk = P.nextbank()
                mmgroup(ps[:, bk, :], [(ones_b, xsq[:, k, :]) for k in range(8)], [('xsq', k) for k in range(8)], [('ps', bk)])
                act(rs[b], ps[:, bk, :], AF.Sqrt, [('ps', bk)], [('rs', b)], bias=eps_t, scale=1.0 / D)
                P.op('dve', lambda v, o=rs[b]: v.reciprocal(out=o, in_=o), [('rs', b)], [('rs', b)])
                for k in range(8):
                    stt('dve', xt[b][:, k, :], xt[b][:, k, :], fing_t[:, k:k + 1], rs[b], ALU.mult, ALU.mult,
                        [('xt', b), ('rs', b), 'small', ('xsq', k)], [('xt', b)])
                dma('sp', yv[:, :, t * 512:(t + 1) * 512], xt[b], [('xt', b)], [('y', t)], f'ys{b}')
        else:
            rx_reset()
            xt = [xf([8, 512]), xf([8, 512])]
            xv = fm(xsrc); yv = fm(yout)
            for t in range(NT):
                b = t % 2
                dma('sp', xt[b], xv[:, :, t * 512:(t + 1) * 512], [('x', t)], [('xt', b)], f'xt{b}')
                dma('sp', yv[:, :, t * 512:(t + 1) * 512], xt[b], [('xt', b)], [('y', t)], f'ys{b}')
        P.barrier()

        import os as _os
        _xpe = int(_os.environ.get('XTRA_PE', '0')); _xdve = int(_os.environ.get('XTRA_DVE', '0'))
        if _xpe:
            def fn(t):
                for i in range(_xpe):
                    ins = t.matmul(ps[:, 0, :], lhsT=ident_b, rhs=RX[:, 0:512], start=True, stop=True)
                return ins
            P.op('pe', fn, (), [('ps', 0)])
        if _xdve:
            def fn(v):
                for i in range(_xdve):
                    ins = v.memset(stat[:, 0:1], 0.0)
                return ins
            P.op('dve', fn, (), ['junkx'])
        _xdma = int(_os.environ.get('XTRA_DMA', '0'))
        if _dmy is not None:
            dma('sp', stat[:, 0:8], _dmy[:, 219000:219008], (), ['junkd'], 'junkd')
        if _xdma:
            rx_reset()
            xtt = xf([8, 512])
            for i in range(_xdma):
                dma('sp', xtt, fm(xin)[:, :, (i % 8) * 512:(i % 8 + 1) * 512], (), ['xtt'], 'xtt')
            P.barrier()
        P.finalize(nc, es)
        block = es.enter_context(nc.Block())

        @block.tensor
        def _(t):
            P.emit(t, 'pe')

        @block.scalar
        def _(a):
            P.emit(a, 'act')

        @block.vector
        def _(v):
            P.emit(v, 'dve')

        @block.gpsimd
        def _(g):
            P.emit(g, 'pool')

        @block.sync
        def _(s):
            P.emit(s, 'sp')
    return nc


_CACHE = {}


def _get_nc(key, *args, **kw):
    if key not in _CACHE:
        _CACHE[key] = build(*args, **kw)
    return _CACHE[key]


def _layer_inputs(inp, layers):
    f = lambda a: np.ascontiguousarray(np.asarray(a, dtype=np.float32))
    g = lambda k: np.asarray(inp[k], dtype=np.float32)
    d = {}
    adab = np.stack([g('ada_b')[l].reshape(48, 128).T for l in layers], axis=1).reshape(128, -1)
    n1g = np.stack([g('norm1_g')[l].reshape(8, 128).T for l in layers], axis=1).reshape(128, -1)
    n2g = np.stack([g('norm2_g')[l].reshape(8, 128).T for l in layers], axis=1).reshape(128, -1)
    cw = np.stack([g('conv_w')[l].reshape(3, 44, 128).transpose(2, 0, 1) for l in layers], axis=1).reshape(128, -1)
    cb = np.stack([g('conv_b')[l].reshape(44, 128).T for l in layers], axis=1).reshape(128, -1)
    fing = g('final_g').reshape(8, 128).T
    d['smalls'] = f(np.concatenate([adab, n1g, n2g, cw, cb, fing], axis=1))
    for i, l in enumerate(layers):
        d[f'wp{i}'] = f(np.concatenate([g('w_in')[l], g('w_fourier')[l], g('w_ret')[l], g('w_out')[l],
                                        g('ffn_up')[l], g('ada_w')[l], g('w_in')[l][:, :1024].T], axis=1))
    d['downs'] = f(np.stack([g('ffn_down')[l] for l in layers], axis=0))
    C = make_consts()
    d['c_128'] = f(np.stack([C['ident'], C['cg'], C['sg'], C['g3c'], C['g3s']], axis=1))
    d['c_tab'] = f(np.concatenate([C['dmaskT'], C['tabf'], C['tabb'], C['decf'], C['decb']], axis=1))
    d['c_rot'] = f(np.stack([C['cosT'], C['sinT']], axis=1))
    d['c_tmat'] = f(C['tmat'])
    return d


FUSED = True


def kernel(**inp):
    x = np.asarray(inp['x'], dtype=np.float32)
    c = np.asarray(inp['c'], dtype=np.float32)
    B = x.shape[0]
    xT = [np.ascontiguousarray(x[b].T) for b in range(B)]
    cvs = [np.ascontiguousarray(c[b].reshape(8, 128).T) for b in range(B)]
    if FUSED:
        nc = _get_nc('fused', [0, 1, 2, 3], True)
        shared = _layer_inputs(inp, [0, 1, 2, 3])
        maps = [dict(shared, xT=xT[b], cvec=cvs[b]) for b in range(B)]
        res = run_bass_kernel_spmd(nc, maps, core_ids=list(range(B)))
        outs = [res.results[b]['yT'] for b in range(B)]
    else:
        cur = xT
        for l in range(4):
            last = (l == 3)
            nc = _get_nc('last' if last else 'layer', [0], last)
            shared = _layer_inputs(inp, [l])
            maps = [dict(shared, xT=cur[b], cvec=cvs[b]) for b in range(B)]
            res = run_bass_kernel_spmd(nc, maps, core_ids=list(range(B)))
            cur = [res.results[b]['yT'] for b in range(B)]
        outs = cur
    return np.stack([np.ascontiguousarray(o.T) for o in outs], axis=0).astype(np.float32)
```

```python
import numpy as np
from contextlib import ExitStack
import concourse.bass as bass
import concourse.mybir as mybir
from concourse.bass_utils import run_bass_kernel_spmd

F32 = mybir.dt.float32
BF16 = mybir.dt.bfloat16
AF = mybir.ActivationFunctionType
ALU = mybir.AluOpType

S = 4096
D = 1024
NT = 8
DFF = 2816
EPS = 1e-6
ENGS = ['pe', 'act', 'dve', 'pool', 'sp']


class Op:
    __slots__ = ('eng', 'fn', 'dma', 'seq', 'signal', 'waits', 'sem', 'sigval', 'dmaval')


class Prog:
    def __init__(self):
        self.eng_ops = {e: [] for e in ENGS}
        self.last_w = {}
        self.readers = {}
        self.known = {e: {} for e in ENGS}
        self.dma_cnt = {}
        self.pending_dma = []
        self.last_compute = {e: None for e in ENGS}
        self.bank = 0
        self.bank_mod = 8

    def nextbank(self, n=1):
        if n == 2 and self.bank % 2:
            self.bank += 1
        b = self.bank % self.bank_mod
        self.bank += n
        return b

    def op(self, eng, fn, reads=(), writes=(), dma=None):
        o = Op()
        o.eng = eng; o.fn = fn; o.dma = dma; o.signal = False; o.waits = []
        o.sem = None; o.sigval = 0; o.dmaval = 0
        o.seq = len(self.eng_ops[eng])
        if dma is not None:
            self.dma_cnt[dma] = self.dma_cnt.get(dma, 0) + 1
            o.dmaval = 16 * self.dma_cnt[dma]
            assert o.dmaval < 60000, dma
        raw = set(); deps = []
        for k in reads:
            w = self.last_w.get(k)
            if w is not None:
                raw.add(id(w)); deps.append(w)
        for k in writes:
            w = self.last_w.get(k)
            if w is not None:
                deps.append(w)
            rd = self.readers.get(k)
            if rd:
                deps.extend(rd.values())
        kn = self.known[eng]
        for d in deps:
            if d is o:
                continue
            if d.dma is not None:
                key = ('d', d.dma)
                tot = 16 * self.dma_cnt[d.dma] - (16 if (dma == d.dma) else 0)
                if kn.get(key, 0) < tot:
                    kn[key] = tot; o.waits.append((d.dma, tot))
            else:
                if d.eng == eng and dma is None and id(d) not in raw:
                    continue
                key = ('e', d.eng)
                if kn.get(key, -1) < d.seq:
                    kn[key] = d.seq; d.signal = True; o.waits.append(d)
        for k in writes:
            self.last_w[k] = o; self.readers[k] = {}
        for k in reads:
            r = self.readers.setdefault(k, {})
            r[eng if dma is None else ('d', id(o))] = o
        self.eng_ops[eng].append(o)
        if dma is not None:
            self.pending_dma.append(o)
        else:
            self.last_compute[eng] = o
        return o

    def barrier(self):
        lasts = dict(self.last_compute)
        pend = self.pending_dma; self.pending_dma = []
        for e in ENGS:
            o = Op()
            o.eng = e; o.fn = None; o.dma = None; o.signal = False; o.waits = []
            o.sem = None; o.sigval = 0; o.dmaval = 0
            o.seq = len(self.eng_ops[e])
            kn = self.known[e]
            for e2, d in lasts.items():
                if d is None or e2 == e:
                    continue
                key = ('e', e2)
                if kn.get(key, -1) < d.seq:
                    kn[key] = d.seq; d.signal = True; o.waits.append(d)
            for d in pend:
                key = ('d', d.dma)
                tot = 16 * self.dma_cnt[d.dma]
                if kn.get(key, 0) < tot:
                    kn[key] = tot; o.waits.append((d.dma, tot))
            self.eng_ops[e].append(o)
        self.last_w = {}; self.readers = {}

    def finalize(self, nc, es):
        n = 0
        for e in ENGS:
            cnt = 0; cur = None
            for o in self.eng_ops[e]:
                if o.dma is None and o.fn is not None and o.signal:
                    if cur is None or cnt >= 30000:
                        cur = es.enter_context(nc.semaphore(f"s_{e}_{n}")); n += 1; cnt = 0
                    cnt += 1; o.sem = cur; o.sigval = cnt
        self.dsems = {name: es.enter_context(nc.semaphore("d_" + name)) for name in self.dma_cnt}

    def emit(self, h, e):
        for o in self.eng_ops[e]:
            for d in o.waits:
                if isinstance(d, tuple):
                    h.wait_ge(self.dsems[d[0]], d[1])
                else:
                    h.wait_ge(d.sem, d.sigval)
            if o.fn is not None:
                ins = o.fn(h)
                if o.dma is not None:
                    ins.then_inc(self.dsems[o.dma], 16)
                elif o.signal:
                    ins.then_inc(o.sem, 1)


def make_consts():
    c = {}
    j = np.arange(128)
    ang = 2.0 * np.pi * ((j[:, None] * j[None, :]) % 128) / 128.0
    c['cg'] = (np.cos(ang) / np.sqrt(128.0)).astype(np.float32)
    c['sg'] = (np.sin(ang) / np.sqrt(128.0)).astype(np.float32)
    c['ident'] = np.eye(128, dtype=np.float32)
    n1 = np.arange(128); k1 = np.arange(128)
    T = np.zeros((32, 128, 3, 128), np.float32)
    for n2 in range(32):
        n = 32 * n1 + n2
        a = 2.0 * np.pi * ((n[:, None] * k1[None, :]) % 4096) / 4096.0
        T[n2, :, 0, :] = np.cos(a); T[n2, :, 1, :] = np.sin(a); T[n2, :, 2, :] = -np.sin(a)
    c['tmat'] = T
    g3c = np.zeros((128, 128), np.float32); g3s = np.zeros((128, 128), np.float32)
    for n2 in range(32):
        for k2 in range(32):
            a = 2.0 * np.pi * ((n2 * k2) % 32) / 32.0
            for b in range(4):
                g3c[n2 * 4 + b, k2 * 4 + b] = np.cos(a) / 64.0
                g3s[n2 * 4 + b, k2 * 4 + b] = -np.sin(a) / 64.0
    c['g3c'] = g3c; c['g3s'] = g3s
    half = 128
    inv_freq = (10000.0 ** (-np.arange(half, dtype=np.float32) / half)).astype(np.float32)
    angr = (np.arange(S, dtype=np.float32)[:, None] * inv_freq[None, :]).astype(np.float32)
    c['cosT'] = np.ascontiguousarray(np.cos(angr).T.astype(np.float32))
    c['sinT'] = np.ascontiguousarray(np.sin(angr).T.astype(np.float32))
    hh = np.arange(4, dtype=np.float64)
    lgf = np.log1p(-np.exp2(-(5.0 + 0.0) - hh))
    lgb = np.log1p(-np.exp2(-(5.0 + 0.5) - hh))
    cc = np.arange(128, dtype=np.float64)
    diff = cc[:, None] - cc[None, :]
    dm = np.zeros((4, 128, 128))
    for h in range(4):
        dm[h] = np.where(diff >= 0, np.exp(lgf[h] * np.maximum(diff, 0)), np.exp(lgb[h] * np.maximum(-diff, 0)))
    c['dmaskT'] = np.ascontiguousarray(np.transpose(dm, (2, 0, 1)) / 16.0).astype(np.float32).reshape(128, 512)
    tabf = np.zeros((128, 8, 128)); tabb = np.zeros((128, 8, 128))
    for ch in range(8):
        h = ch // 2
        tabf[:, ch, :] = (np.exp(lgf[h] * (cc + 1.0)) / 16.0)[None, :]
        tabb[:, ch, :] = (np.exp(lgb[h] * (128.0 - cc)) / 16.0)[None, :]
    c['tabf'] = tabf.astype(np.float32).reshape(128, 1024)
    c['tabb'] = tabb.astype(np.float32).reshape(128, 1024)
    decf = np.zeros((128, 1024)); decb = np.zeros((128, 1024))
    for h in range(4):
        decf[:, h * 256:(h + 1) * 256] = np.exp(lgf[h] * (127.0 - cc))[:, None]
        decb[:, h * 256:(h + 1) * 256] = np.exp(lgb[h] * cc)[:, None]
    c['decf'] = decf.astype(np.float32); c['decb'] = decb.astype(np.float32)
    c['_gcf'] = [float(np.exp(lgf[h] * 128.0)) for h in range(4)]
    c['_gcb'] = [float(np.exp(lgb[h] * 128.0)) for h in range(4)]
    return c


CONST_SHAPES = {'cg': [128, 128], 'sg': [128, 128], 'ident': [128, 128], 'tmat': [32, 128, 3, 128],
                'g3c': [128, 128], 'g3s': [128, 128], 'cosT': [128, S], 'sinT': [128, S],
                'dmaskT': [128, 512], 'tabf': [128, 1024], 'tabb': [128, 1024],
                'decf': [128, 1024], 'decb': [128, 1024]}


def build(layers, final, stop_after=None, dbg=()):
    nc = bass.Bass("TRN2", target_bir_lowering=False)
    C = make_consts()
    gcf = C['_gcf']; gcb = C['_gcb']
    NL = len(layers)

    def din(name, shape):
        return nc.dram_tensor(name, shape, F32, kind="ExternalInput").ap()

    xin = din("xT", [D, S])
    cv = din("cvec", [128, 8])
    SMW = NL * 48 + NL * 8 + NL * 8 + NL * 132 + NL * 44 + 8
    smalls = din("smalls", [128, SMW])
    WPC = 7168 + 3 * D + 2 * DFF + 6 * D + D
    wps = [din(f"wp{li}", [D, WPC]) for li in range(NL)]
    downs = din("downs", [NL, DFF, D])
    W = {}
    for li in range(NL):
        wpv = wps[li].rearrange("(k p) n -> p k n", p=128)
        o = [0]

        def cut(n, wpv=wpv, o=o):
            a_ = wpv[:, :, o[0]:o[0] + n]; o[0] += n; return a_
        W[li] = dict(w_in=cut(7168), w_f=cut(D), w_ret=cut(D), w_out=cut(D), up=cut(2 * DFF), ada=cut(6 * D),
                     winfT=cut(D), down=downs[li])
    c128 = din("c_128", [128, 5, 128])
    ctab = din("c_tab", [128, 512 + 4 * 1024])
    crot = din("c_rot", [128, 2, S])
    K = {'tmat': din("c_tmat", [32, 128, 3, 128]),
         'ident': c128[:, 0, :], 'cg': c128[:, 1, :], 'sg': c128[:, 2, :], 'g3c': c128[:, 3, :], 'g3s': c128[:, 4, :],
         'dmaskT': ctab[:, 0:512], 'tabf': ctab[:, 512:1536], 'tabb': ctab[:, 1536:2560],
         'decf': ctab[:, 2560:3584], 'decb': ctab[:, 3584:4608],
         'cosT': crot[:, 0, :], 'sinT': crot[:, 1, :]}
    _dmy = None
    yout = nc.dram_tensor("yT", [D, S], F32, kind="ExternalOutput").ap()

    def dscr(name, shape, dt):
        kind = "ExternalOutput" if name in dbg else "Internal"
        return nc.dram_tensor(name, shape, dt, kind=kind).ap()

    xS = dscr("xS", [D, S], F32)
    Bd = dscr("Bd", [2, 32, 128, 512], BF16)
    ZfT = dscr("ZfT", [D, S], BF16)
    yrTd = dscr("yrTd", [D, S], BF16)
    Sbd = dscr("Sbd", [32, 128, 2048], BF16)
    kTd = dscr("kTd", [16, 128, 2048], BF16)
    vd = dscr("vd", [32, 128, 1024], BF16)
    hTd = dscr("hTd", [D, S], BF16) if "hTd" in dbg else None

    def fm(ap):
        return ap.rearrange("(k p) n -> p k n", p=128)

    P = Prog()
    es = ExitStack()
    with es:
        RH = es.enter_context(nc.sbuf_tensor("RH", [128, 32768], BF16))
        RW = es.enter_context(nc.sbuf_tensor("RW", [128, 33792], BF16))
        RX = es.enter_context(nc.sbuf_tensor("RX", [128, 32768], BF16))
        RXf = RX.bitcast(F32)
        ps = es.enter_context(nc.psum_tensor("ps", [128, 8, 512], F32))
        psb = ps.bitcast(BF16)
        smallf = es.enter_context(nc.sbuf_tensor("smallf", [128, 1536], F32))
        smallb = es.enter_context(nc.sbuf_tensor("smallb", [128, 2048], BF16))
        hT = RH[:, :].rearrange("p (k n) -> p k n", k=8)

        so = [0]

        def sf(n):
            a = smallf[:, so[0]:so[0] + n]; so[0] += n; return a
        cvt = sf(8); adab_t = sf(NL * 48); n1g_t = sf(NL * 8); n2g_t = sf(NL * 8)
        cw_t = sf(NL * 132); cb_t = sf(NL * 44); fing_t = sf(8)
        modT = sf(48); prm = sf(16); eps_t = sf(1); stat = sf(32)
        assert so[0] <= 1536
        adab_v = adab_t.rearrange("p (l j) -> p l j", l=NL)
        n1g_v = n1g_t.rearrange("p (l j) -> p l j", l=NL)
        n2g_v = n2g_t.rearrange("p (l j) -> p l j", l=NL)
        cw_v = cw_t.rearrange("p (l i j) -> p l i j", l=NL, i=3)
        cb_v = cb_t.rearrange("p (l j) -> p l j", l=NL)
        bo = [0]

        def sb(n):
            a = smallb[:, bo[0]:bo[0] + n]; bo[0] += n; return a
        cact = sb(8); ones_b = sb(128); ident_b = sb(128); cg_b = sb(128); sg_b = sb(128)
        g3c_b = sb(128); g3s_b = sb(128)
        assert bo[0] <= 2048

        rx = [0]

        def rx_reset():
            rx[0] = 0

        def xb(shape):
            n = int(np.prod(shape)); o = rx[0] // 2; rx[0] += n * 2
            assert rx[0] <= 65536, rx[0]
            a = RX[:, o:o + n]
            if len(shape) == 2:
                a = a.rearrange("p (a b) -> p a b", a=shape[0])
            elif len(shape) == 3:
                a = a.rearrange("p (a b c) -> p a b c", a=shape[0], b=shape[1])
            return a

        def xf(shape):
            rx[0] = (rx[0] + 3) // 4 * 4
            n = int(np.prod(shape)); o = rx[0] // 4; rx[0] += n * 4
            assert rx[0] <= 65536, rx[0]
            a = RXf[:, o:o + n]
            if len(shape) == 2:
                a = a.rearrange("p (a b) -> p a b", a=shape[0])
            elif len(shape) == 3:
                a = a.rearrange("p (a b c) -> p a b c", a=shape[0], b=shape[1])
            return a

        def wv(off, shape):
            n = int(np.prod(shape))
            assert off + n <= 33792
            a = RW[:, off:off + n]
            if len(shape) == 2:
                a = a.rearrange("p (a b) -> p a b", a=shape[0])
            elif len(shape) == 3:
                a = a.rearrange("p (a b c) -> p a b c", a=shape[0], b=shape[1])
            return a

        def dma(eng, out, in_, reads, writes, name):
            P.op(eng, lambda h, out=out, in_=in_: h.dma_start(out=out, in_=in_), reads, writes, dma=name)

        def mmgroup(out, pairs, reads, writes):
            def fn(t, out=out, pairs=pairs):
                n = len(pairs)
                for i, (l, r) in enumerate(pairs):
                    ins = t.matmul(out, lhsT=l, rhs=r, start=(i == 0), stop=(i == n - 1))
                return ins
            P.op('pe', fn, reads, writes)

        def act(out, in_, func, reads, writes, bias=None, scale=None, accum=None):
            def fn(a, out=out, in_=in_, func=func, bias=bias, scale=scale, accum=accum):
                kw = {}
                if bias is not None: kw['bias'] = bias
                if scale is not None: kw['scale'] = scale
                if accum is not None: kw['accum_out'] = accum
                return a.activation(out=out, in_=in_, func=func, **kw)
            P.op('act', fn, reads, writes)

        def tt(eng, out, in0, in1, op, reads, writes):
            P.op(eng, lambda v, out=out, in0=in0, in1=in1, op=op: v.tensor_tensor(out=out, in0=in0, in1=in1, op=op),
                 reads, writes)

        def stt(eng, out, in0, scalar, in1, op0, op1, reads, writes):
            P.op(eng, lambda v, out=out, in0=in0, scalar=scalar, in1=in1, op0=op0, op1=op1:
                 v.scalar_tensor_tensor(out=out, in0=in0, scalar=scalar, in1=in1, op0=op0, op1=op1), reads, writes)

        def ts(eng, out, in0, s1, s2, op0, op1, reads, writes):
            def fn(v, out=out, in0=in0, s1=s1, s2=s2, op0=op0, op1=op1):
                if s2 is None:
                    return v.tensor_scalar(out=out, in0=in0, scalar1=s1, scalar2=None, op0=op0)
                return v.tensor_scalar(out=out, in0=in0, scalar1=s1, scalar2=s2, op0=op0, op1=op1)
            P.op(eng, fn, reads, writes)

        def copy(eng, out, in_, reads, writes):
            if eng == 'act':
                P.op('act', lambda a, out=out, in_=in_: a.copy(out=out, in_=in_), reads, writes)
            else:
                P.op(eng, lambda v, out=out, in_=in_: v.tensor_copy(out=out, in_=in_), reads, writes)

        def memset(eng, ap, val, writes):
            P.op(eng, lambda v, ap=ap, val=val: v.memset(ap, val), (), writes)

        def wload(dst, src, name=None, key='RW'):
            if name is None:
                name = 'W' + ('_'.join(str(k_) for k_ in key) if isinstance(key, tuple) else str(key))
            dma('pool', dst, src, (), [key], name)

        TK = [('hT', t) for t in range(NT)]

        dma('sp', cvt, cv, (), ['small'], 'small')
        dma('sp', smallf[:, 8:8 + SMW], smalls, (), ['small'], 'small')
        for nm, dst in (('ident', ident_b), ('cg', cg_b), ('sg', sg_b), ('g3c', g3c_b), ('g3s', g3s_b)):
            dma('pool', dst, K[nm], (), ['smallb'], 'smallb')
        memset('dve', ones_b, 1.0, ['smallb'])
        memset('dve', eps_t, EPS, ['small'])
        act(cact, cvt, AF.Silu, ['small'], ['cact'])
        P.barrier()

        xsrc = xin
        for li in range(NL):
            Wl = W[li]
            for half in range(2):
                Wa = wv(0, [8, 3072])
                wload(Wa, Wl['ada'][:, :, half * 3072:(half + 1) * 3072])

                def fn(t, Wa=Wa, half=half):
                    for jj in range(24):
                        j = half * 24 + jj
                        for k in range(8):
                            ins = t.matmul(ps[:, 0, j:j + 1], lhsT=Wa[:, k, jj * 128:(jj + 1) * 128],
                                           rhs=cact[:, k:k + 1], start=(k == 0), stop=(k == 7))
                    return ins
                P.op('pe', fn, ['RW', 'cact'], [('ps', 0)])
            tt('dve', modT, ps[:, 0, 0:48], adab_v[:, li, :], ALU.add, [('ps', 0), 'small'], ['modT'])
            stt('dve', prm[:, 0:8], modT[:, 8:16], 1.0, n1g_v[:, li, :], ALU.add, ALU.mult, ['modT', 'small'], ['prm'])
            stt('dve', prm[:, 8:16], modT[:, 32:40], 1.0, n2g_v[:, li, :], ALU.add, ALU.mult, ['modT', 'small'], ['prm'])
            P.barrier()
            A1 = prm[:, 0:8]; A2 = prm[:, 8:16]
            B1 = modT[:, 0:8]; G1 = modT[:, 16:24]; B2 = modT[:, 24:32]; G2 = modT[:, 40:48]

            def phase_norm(xsrc, A, B):
                rx_reset()
                xt = [xf([8, 512]), xf([8, 512])]
                xsq = xb([8, 512])
                rs = [xf([512]), xf([512])]
                tmp = [xf([512]), xf([512])]
                xv = fm(xsrc)
                for t in range(NT):
                    b = t % 2
                    dma('sp', xt[b], xv[:, :, t * 512:(t + 1) * 512], [('x', t)], [('xt', b)], f'xt{b}')
                    for k in range(8):
                        act(xsq[:, k, :], xt[b][:, k, :], AF.Square, [('xt', b)], [('xsq', k)])
                    bk = P.nextbank()
                    mmgroup(ps[:, bk, :], [(ones_b, xsq[:, k, :]) for k in range(8)],
                            [('xsq', k) for k in range(8)], [('ps', bk)])
                    act(rs[b], ps[:, bk, :], AF.Sqrt, [('ps', bk)], [('rs', b)], bias=eps_t, scale=1.0 / D)
                    P.op('dve', lambda v, o=rs[b]: v.reciprocal(out=o, in_=o), [('rs', b)], [('rs', b)])
                    for k in range(8):
                        stt('dve', tmp[k % 2], xt[b][:, k, :], A[:, k:k + 1], rs[b], ALU.mult, ALU.mult,
                            [('xt', b), ('rs', b), 'prm'], [('tmp', k % 2)])
                        act(hT[:, k, t * 512:(t + 1) * 512], tmp[k % 2], AF.Identity,
                            [('tmp', k % 2), 'modT'], [('hT', t)], bias=B[:, k:k + 1], scale=1.0)
                P.barrier()

            phase_norm(xsrc, A1, B1)
            if hTd is not None:
                dma('sp', fm(hTd), hT, TK, ['hTd'], 'dbg')
                P.barrier()
            if stop_after == 'norm1':
                break

            rx_reset()
            Wf_b = xb([8, 1024]); WiT_b = xb([8, 1024])
            Wcs = wv(0, [8, 2048])
            M2 = wv(16384, [8, 2048])
            dma('pool', Wf_b, Wl['w_f'], (), ['Wf'], 'Wf')
            dma('pool', WiT_b, Wl['winfT'], (), ['WiT'], 'WiT')
            ev = 0
            for g in range(8):
                for cs in range(2):
                    for ot in range(2):
                        bk = P.nextbank()
                        mmgroup(ps[:, bk, :], [((cg_b if cs == 0 else sg_b), Wf_b[:, g, ot * 512:(ot + 1) * 512])],
                                ['Wf', 'smallb'], [('ps', bk)])
                        copy('act' if ev % 2 else 'dve', M2[:, g, cs * 1024 + ot * 512: cs * 1024 + (ot + 1) * 512],
                             ps[:, bk, :], [('ps', bk)], ['M2'])
                        ev += 1
            for m in range(8):
                for cs in range(2):
                    for ot in range(2):
                        bk = P.nextbank()
                        mmgroup(ps[:, bk, :],
                                [(WiT_b[:, c, m * 128:(m + 1) * 128], M2[:, c, cs * 1024 + ot * 512: cs * 1024 + (ot + 1) * 512])
                                 for c in range(8)], ['WiT', 'M2'], [('ps', bk)])
                        for q in range(2):
                            ct4 = 2 * ot + q
                            copy('act' if ev % 2 else 'dve', Wcs[:, m, ct4 * 512 + cs * 256: ct4 * 512 + (cs + 1) * 256],
                                 ps[:, bk, q * 256:(q + 1) * 256], [('ps', bk)], ['Wcs'])
                            ev += 1
            P.barrier()

            rx_reset()
            Dt = [xb([512]), xb([512])]
            Bt = [xb([512]), xb([512])]
            Tt = [xb([3, 128]), xb([3, 128])]
            Zt = xb([2, 4096])
            Zt4 = Zt.rearrange("p c (j k) -> p c j k", k=32)
            Bp = wv(16384, [32, 512])
            def f_stage01(ct4, n2):
                    b = n2 % 2
                    dma('pool', Tt[b], K['tmat'][n2], (), [('Tt', b)], f'Tt{b}')
                    bk = P.nextbank()
                    mmgroup(ps[:, bk, :], [(hT[:, m, n2::32], Wcs[:, m, ct4 * 512:(ct4 + 1) * 512]) for m in range(8)],
                            TK + ['Wcs'], [('ps', bk)])
                    copy('act', Dt[b], ps[:, bk, :], [('ps', bk)], [('Dt', b)])
                    bk2 = P.nextbank()

                    def fn(t, bk2=bk2, b=b):
                        t.matmul(ps[:, bk2, 0:256], lhsT=Tt[b][:, 0, :], rhs=Dt[b][:, 0:256], start=True, stop=False)
                        t.matmul(ps[:, bk2, 0:256], lhsT=Tt[b][:, 2, :], rhs=Dt[b][:, 256:512], start=False, stop=True)
                        t.matmul(ps[:, bk2, 256:512], lhsT=Tt[b][:, 0, :], rhs=Dt[b][:, 256:512], start=True, stop=False)
                        return t.matmul(ps[:, bk2, 256:512], lhsT=Tt[b][:, 1, :], rhs=Dt[b][:, 0:256], start=False, stop=True)
                    P.op('pe', fn, [('Tt', b), ('Dt', b)], [('ps', bk2)])
                    copy('dve', Bt[b], ps[:, bk2, :], [('ps', bk2)], [('Bt', b)])
                    dma('sp', Bd[ct4 % 2, n2], Bt[b], [('Bt', b)], [('Bd', ct4 % 2, n2 // 16)], f'Bts{b}')

            def f_load(ct4):
                Bdv = Bd[ct4 % 2].rearrange("n (a k) c -> (n a) k c", a=4)
                for hf in range(2):
                    dma('sp', Bp[:, hf * 16:(hf + 1) * 16, :], Bdv[:, hf * 16:(hf + 1) * 16, :],
                        [('Bd', ct4 % 2, 0), ('Bd', ct4 % 2, 1)], [('Bp', hf)], f'Bp{hf}')

            def f_stage3(ct4):
                for chh in range(2):
                    for k0 in range(0, 32, 4):
                        bk = P.nextbank()

                        def fn(t, bk=bk, k0=k0, chh=chh):
                            for q in range(4):
                                kk = k0 + q
                                t.matmul(ps[:, bk, q * 128:(q + 1) * 128], lhsT=Bp[:, kk, chh * 128:(chh + 1) * 128],
                                         rhs=g3c_b, start=True, stop=False)
                                ins = t.matmul(ps[:, bk, q * 128:(q + 1) * 128],
                                               lhsT=Bp[:, kk, 256 + chh * 128:256 + (chh + 1) * 128],
                                               rhs=g3s_b, start=False, stop=True)
                            return ins
                        P.op('pe', fn, [('Bp', k0 // 16), 'smallb'], [('ps', bk)])
                        copy('act' if (k0 // 4) % 2 else 'dve', Zt4[:, chh, :, k0:k0 + 4],
                             ps[:, bk, :].rearrange("p (q j) -> p j q", q=4), [('ps', bk)], ['Zt'])
                dma('sp', fm(ZfT)[:, ct4 * 2:ct4 * 2 + 2, :], Zt, ['Zt'], [('ZfT', t) for t in range(NT)], 'Zts')

            for ct4 in range(4):
                for n2 in range(32):
                    if n2 == 0 and ct4 > 0:
                        f_load(ct4 - 1)
                    if n2 == 8 and ct4 > 0:
                        f_stage3(ct4 - 1)
                    f_stage01(ct4, n2)
            f_load(3)
            f_stage3(3)
            P.barrier()
            if stop_after == 'fourier':
                break

            rx_reset()
            Wqkvg = wv(0, [8, 4096])
            for i in (1, 2, 0, 3):
                for hf in range(2):
                    c0_ = i * 1024 + hf * 512
                    wload(Wqkvg[:, :, c0_:c0_ + 512], Wl['w_in'][:, :, 1024 + c0_: 1024 + c0_ + 512], key=('Wr', i, hf))
            RT = 256; NRT = S // RT; CPT = RT // 128
            cst = [xf([RT]), xf([RT])]
            snt = [xf([RT]), xf([RT])]
            tf1 = xf([RT]); tf2 = xf([RT])
            pa = xf([RT]); pb_ = xf([RT]); pc = pa; pd = pb_
            kT = xb([8, RT]); qT = xb([8, RT])
            kd = xb([1024]); v2 = [xb([1024]), xb([1024])]
            Sst = xf([4, 512])
            Sbf = xb([4, 512])
            Sbl = [xb([4, 512])] * 2
            scT = xb([512]); sg2 = [xb([1024]), xb([1024])]; yn2 = [xb([1024]), xb([1024])]; yn = yn2[0]
            yrT = xb([8, RT])
            qdf = xb([8, 128]); qdb = xb([8, 128])
            tabf_b = xb([8, 128]); tabb_b = xb([8, 128]); decf_b = xb([1024]); decb_b = xb([1024]); dmask_b = xb([512])
            junk = None
            dma('pool', tabf_b.rearrange("p a b -> p (a b)"), K['tabf'], (), ['rc'], 'rc')
            dma('pool', tabb_b.rearrange("p a b -> p (a b)"), K['tabb'], (), ['rc'], 'rc')
            dma('pool', decf_b, K['decf'], (), ['rc'], 'rc')
            dma('pool', decb_b, K['decb'], (), ['rc'], 'rc')
            dma('pool', dmask_b, K['dmaskT'], (), ['rc'], 'rc')
            s1 = stat[:, 0:4]; s2 = stat[:, 4:8]; mean = stat[:, 8:12]; msq = stat[:, 12:16]
            var = stat[:, 16:20]; rstd = stat[:, 20:24]; nbias = stat[:, 24:28]

            def rotary_proj(T, col0, dst, dkey):
                b = T % 2
                for h in range(4):
                    bks = []
                    for q in range(2):
                        ch = 2 * h + q
                        bk = P.nextbank(); bks.append(bk)
                        mmgroup(ps[:, bk, 0:RT],
                                [(Wqkvg[:, m, col0 + ch * 128: col0 + (ch + 1) * 128], hT[:, m, T * RT:(T + 1) * RT])
                                 for m in range(8)], [('hT', (T * RT) // 512), ('Wr', col0 // 1024, ch // 4)], [('ps', bk)])
                    copy('act', tf1, ps[:, bks[0], 0:RT], [('ps', bks[0])], ['tf1'])
                    copy('act', tf2, ps[:, bks[1], 0:RT], [('ps', bks[1])], ['tf2'])
                    tt('dve', pa, tf1, cst[b], ALU.mult, ['tf1', ('cs', b)], ['pa'])
                    tt('dve', pb_, tf2, snt[b], ALU.mult, ['tf2', ('cs', b)], ['pb'])
                    tt('dve', dst[:, 2 * h, :], pa, pb_, ALU.subtract, ['pa', 'pb'], [dkey])
                    tt('dve', pc, tf1, snt[b], ALU.mult, ['tf1', ('cs', b), 'pa'], ['pa'])
                    tt('dve', pd, tf2, cst[b], ALU.mult, ['tf2', ('cs', b), 'pb'], ['pb'])
                    tt('dve', dst[:, 2 * h + 1, :], pc, pd, ALU.add, ['pa', 'pb'], [dkey])

            def load_cs(T):
                b = T % 2
                dma('sp', cst[b], K['cosT'][:, T * RT:(T + 1) * RT], (), [('cs', b)], f'cs{b}')
                dma('sp', snt[b], K['sinT'][:, T * RT:(T + 1) * RT], (), [('cs', b)], f'cs{b}')

            def kv_chunk(T, c4, dec_b, fwd):
                n = T * CPT + c4
                v_t = v2[n % 2]; vk = ('v_t', n % 2)
                if not fwd:
                    bk = P.nextbank(2)
                    for ot in range(2):
                        mmgroup(ps[:, bk + ot, :],
                                [(hT[:, m, n * 128:(n + 1) * 128], Wqkvg[:, m, 2048 + ot * 512: 2048 + (ot + 1) * 512])
                                 for m in range(8)], [('hT', n // 4), ('Wr', 2, ot)], [('ps', bk + ot)])
                    copy('act', v_t.rearrange("p (a b) -> p a b", a=2), ps[:, bk:bk + 2, :],
                         [('ps', bk), ('ps', bk + 1)], [vk])
                    dma('act', vd[n], v_t, [vk], [('vd', n)], f'vts{n % 2}')
                else:
                    dma('sp', v_t, vd[n], [('vd', n)], [vk], f'vtl{n % 2}')
                bk = P.nextbank()

                def fn(t, bk=bk, c4=c4):
                    for j in range(8):
                        ins = t.transpose(psb[:, bk, j * 128:(j + 1) * 128], kT[:, j, c4 * 128:(c4 + 1) * 128], ident_b)
                    return ins
                P.op('pe', fn, ['kT', 'smallb'], [('ps', bk)])
                tt('dve', kd, psb[:, bk, :], dec_b, ALU.mult, [('ps', bk), 'rc'], ['kd'])
                return v_t, vk

            def state_update(gc, v_t, vk):
                for h in range(4):
                    bk = P.nextbank()

                    def fn(t, bk=bk, h=h, v_t=v_t):
                        for dc in range(2):
                            ins = t.matmul(ps[:, bk, dc * 256:(dc + 1) * 256],
                                           lhsT=kd[:, h * 256 + dc * 128: h * 256 + (dc + 1) * 128],
                                           rhs=v_t[:, h * 256:(h + 1) * 256], start=True, stop=True)
                        return ins
                    P.op('pe', fn, ['kd', vk], [('ps', bk)])
                    stt('dve', Sst[:, h, :], Sst[:, h, :], gc[h], ps[:, bk, :], ALU.mult, ALU.add,
                        [('ps', bk), 'Sst'], ['Sst'])

            memset('dve', Sst, 0.0, ['Sst'])
            for T in range(NRT - 1, -1, -1):
                load_cs(T)
                rotary_proj(T, 1024, kT, 'kT')
                dma('pool', kTd[T], kT.rearrange("p a b -> p (a b)"), ['kT'], [('kTd', T)], 'kTs')
                for c4 in range(CPT - 1, -1, -1):
                    n = T * CPT + c4
                    v_t, vk = kv_chunk(T, c4, decb_b, False)
                    copy('dve', Sbl[0], Sst, ['Sst'], [('Sbl', 0)])
                    dma('pool', Sbd[n], Sbl[0].rearrange("p a b -> p (a b)"), [('Sbl', 0)], [('Sbd', n)], 'Sbs0')
                    state_update(gcb, v_t, vk)
            memset('dve', Sst, 0.0, ['Sst'])
            memset('dve', Sbf, 0.0, ['Sbf'])
            P.bank_mod = 4
            ctx = {}

            def fA(T, c4):
                n = T * CPT + c4
                cs_ = slice(c4 * 128, (c4 + 1) * 128)
                if c4 == 0:
                    load_cs(T)
                    rotary_proj(T, 0, qT, 'qT')
                    dma('sp', kT.rearrange("p a b -> p (a b)"), kTd[T], [('kTd', T)], ['kT'], 'kTl')
                v_t, vk = kv_chunk(T, c4, decf_b, True)
                dma('sp', Sbl[0].rearrange("p a b -> p (a b)"), Sbd[n], [('Sbd', n)], [('Sbl', 0)], 'Sbl0')
                sg_t = sg2[n % 2]; sgk = ('sg', n % 2)
                bk = P.nextbank(2)
                for ot in range(2):
                    mmgroup(ps[:, bk + ot, :],
                            [(hT[:, m, n * 128:(n + 1) * 128], Wqkvg[:, m, 3072 + ot * 512: 3072 + (ot + 1) * 512])
                             for m in range(8)], [('hT', n // 4), ('Wr', 3, ot)], [('ps', bk + ot)])
                act(sg_t.rearrange("p (a b) -> p a b", a=2), ps[:, bk:bk + 2, :], AF.Silu,
                    [('ps', bk), ('ps', bk + 1)], [sgk])
                bk = P.nextbank()

                def fn(t, bk=bk, cs_=cs_):
                    for h in range(4):
                        for dc in range(2):
                            ins = t.matmul(ps[:, bk, h * 128:(h + 1) * 128], lhsT=kT[:, 2 * h + dc, cs_],
                                           rhs=qT[:, 2 * h + dc, cs_], start=(dc == 0), stop=(dc == 1))
                    return ins
                P.op('pe', fn, ['kT', 'qT'], [('ps', bk)])
                tt('dve', scT, ps[:, bk, :], dmask_b, ALU.mult, [('ps', bk), 'rc'], ['scT'])
                tt('dve', qdf, qT[:, :, cs_], tabf_b, ALU.mult, ['qT', 'rc'], ['qdf'])
                tt('dve', qdb, qT[:, :, cs_], tabb_b, ALU.mult, ['qT', 'rc'], ['qdb'])
                bky = 4 + 2 * (n % 2)

                def fn(t, bky=bky, n=n, v_t=v_t):
                    for h in range(4):
                        o = ps[:, bky + h // 2, (h % 2) * 256:(h % 2 + 1) * 256]
                        t.matmul(o, lhsT=scT[:, h * 128:(h + 1) * 128], rhs=v_t[:, h * 256:(h + 1) * 256],
                                 start=True, stop=False)
                        for dc in range(2):
                            t.matmul(o, lhsT=qdf[:, 2 * h + dc, :], rhs=Sbf[:, h, dc * 256:(dc + 1) * 256],
                                     start=False, stop=False)
                        for dc in range(2):
                            ins = t.matmul(o, lhsT=qdb[:, 2 * h + dc, :], rhs=Sbl[0][:, h, dc * 256:(dc + 1) * 256],
                                           start=False, stop=(dc == 1))
                    return ins
                P.op('pe', fn, ['scT', vk, 'qdf', 'qdb', 'Sbf', ('Sbl', 0)], [('ps', bky), ('ps', bky + 1)])
                state_update(gcf, v_t, vk)
                copy('dve', Sbf, Sst, ['Sst'], ['Sbf'])
                ctx[n] = (bky, sg_t, sgk)

            def fB1(T, c4):
                n = T * CPT + c4
                bky, sg_t, sgk = ctx[n]
                yn_ = yn2[n % 2]; ynk = ('yn', n % 2); junk_ = yn_[:, 0:256]
                yk = [('ps', bky), ('ps', bky + 1)]
                for h in range(4):
                    o = ps[:, bky + h // 2, (h % 2) * 256:(h % 2 + 1) * 256]
                    act(junk_, o, AF.Copy, yk, [ynk, 'st1'], accum=s1[:, h:h + 1])
                    act(junk_, o, AF.Square, yk, [ynk, 'st1'], accum=s2[:, h:h + 1])
                act(msq, s1, AF.Square, ['st1'], ['msq'], scale=1.0 / 256.0)
                act(mean, msq, AF.Identity, ['msq'], ['mean'], bias=eps_t, scale=-1.0)
                for h in range(4):
                    act(var[:, h:h + 1], s2[:, h:h + 1], AF.Sqrt, ['st1', 'mean'], ['var'], bias=mean[:, h:h + 1],
                        scale=1.0 / 256.0)
                P.op('dve', lambda v: v.reciprocal(out=rstd, in_=var), ['var'], ['rstd'])
                stt('dve', nbias, s1, -1.0 / 256.0, rstd, ALU.mult, ALU.mult, ['st1', 'rstd'], ['nbias'])
                for h in range(4):
                    o = ps[:, bky + h // 2, (h % 2) * 256:(h % 2 + 1) * 256]
                    ts('dve', yn_[:, h * 256:(h + 1) * 256], o, rstd[:, h:h + 1], nbias[:, h:h + 1], ALU.mult, ALU.add,
                       yk + ['rstd', 'nbias'], [ynk])
                tt('dve', yn_, yn_, sg_t, ALU.mult, [ynk, sgk], [ynk])

            def fB2(T, c4):
                n = T * CPT + c4
                cs_ = slice(c4 * 128, (c4 + 1) * 128)
                ctx.pop(n)
                yn_ = yn2[n % 2]; ynk = ('yn', n % 2)
                bk = P.nextbank()

                def fn(t, bk=bk, yn_=yn_):
                    for j in range(8):
                        ins = t.transpose(psb[:, bk, j * 128:(j + 1) * 128], yn_[:, j * 128:(j + 1) * 128], ident_b)
                    return ins
                P.op('pe', fn, [ynk, 'smallb'], [('ps', bk)])
                copy('act', yrT[:, :, cs_], psb[:, bk, :].rearrange("p (a b) -> p a b", a=8), [('ps', bk)], ['yrT'])
                if c4 == CPT - 1:
                    dma('act', fm(yrTd)[:, :, T * RT:(T + 1) * RT], yrT, ['yrT'], [('yrTd', T)], 'yrs')

            seq = [(T, c4) for T in range(NRT) for c4 in range(CPT)]
            NS = len(seq)
            for i in range(NS + 2):
                if i < NS:
                    fA(*seq[i])
                if 1 <= i <= NS:
                    fB1(*seq[i - 1])
                if 2 <= i <= NS + 1:
                    fB2(*seq[i - 2])
            P.bank_mod = 8
            P.barrier()
            if stop_after == 'ret':
                break

            rx_reset()
            Wm = wv(0, [8, 4096])
            for ob in range(4):
                o0 = ob * 256
                wload(Wm[:, :, o0:o0 + 256], Wl['w_in'][:, :, 5120 + o0:5120 + o0 + 256], key=('Wm', 0, ob))
                wload(Wm[:, :, 1024 + o0:1024 + o0 + 256], Wl['w_in'][:, :, 6144 + o0:6144 + o0 + 256], key=('Wm', 1, ob))
                wload(Wm[:, :, 2048 + o0:2048 + o0 + 256], Wl['w_ret'][:, :, o0:o0 + 256], key=('Wm', 2, ob))
            wload(Wm[:, :, 3072:4096], Wl['w_out'], key=('Wm', 3))
            xt1 = xf([8, 512]); xt = [xt1, xt1]
            zf1 = xb([8, 512]); zf = [zf1, zf1]
            yr1 = xb([8, 512]); yr = [yr1, yr1]
            mg = xb([8, 512])
            saf = [xf([512]), xf([512])]; sar = [xf([512]), xf([512])]
            m11 = xf([512]); m21 = xf([512]); m1 = [m11, m11]; m2 = [m21, m21]
            nxsq = xb([8, 512]); nrs = xf([512]); ntmp1 = xf([512]); ntmp = [ntmp1, ntmp1]
            xv = fm(xsrc); xo = fm(xS)
            for T in range(NT):
                b = T % 2
                tsl = slice(T * 512, (T + 1) * 512)
                dma('sp', xt[b], xv[:, :, tsl], [('x', T)], [('xt', 0)], 'xt0')
                dma('sp', zf[b], fm(ZfT)[:, :, tsl], [('ZfT', T)], [('zf', 0)], 'zf0')
                dma('sp', yr[b], fm(yrTd)[:, :, tsl], [('yrTd', 2 * T), ('yrTd', 2 * T + 1)], [('yr', 0)], 'yr0')
                for oc in range(8):
                    q = oc % 2
                    osl = slice(oc * 128, (oc + 1) * 128)
                    bk = P.nextbank()
                    mmgroup(ps[:, bk, :], [(Wm[:, m, osl], hT[:, m, tsl]) for m in range(8)],
                            [('hT', T), ('Wm', 0, oc // 2)], [('ps', bk)])
                    act(saf[q], ps[:, bk, :], AF.Sigmoid, [('ps', bk)], [('saf', q)])
                    bk = P.nextbank()
                    mmgroup(ps[:, bk, :], [(Wm[:, m, 1024 + oc * 128: 1024 + (oc + 1) * 128], hT[:, m, tsl]) for m in range(8)],
                            [('hT', T), ('Wm', 1, oc // 2)], [('ps', bk)])
                    act(sar[q], ps[:, bk, :], AF.Sigmoid, [('ps', bk)], [('sar', q)])
                    bk = P.nextbank()
                    mmgroup(ps[:, bk, :], [(Wm[:, m, 2048 + oc * 128: 2048 + (oc + 1) * 128], yr[b][:, m, :]) for m in range(8)],
                            [('yr', 0), ('Wm', 2, oc // 2)], [('ps', bk)])
                    tt('dve', m1[q], saf[q], zf[b][:, oc, :], ALU.mult, [('saf', q), ('zf', 0)], [('m1', 0)])
                    tt('dve', m2[q], ps[:, bk, :], sar[q], ALU.mult, [('ps', bk), ('sar', q)], [('m2', 0)])
                    tt('dve', mg[:, oc, :], m1[q], m2[q], ALU.add, [('m1', 0), ('m2', 0)], ['mg'])
                for oc in range(8):
                    bk = P.nextbank()
                    mmgroup(ps[:, bk, :], [(Wm[:, m, 3072 + oc * 128: 3072 + (oc + 1) * 128], mg[:, m, :]) for m in range(8)],
                            ['mg', ('Wm', 3)], [('ps', bk)])
                    stt('dve', xt[b][:, oc, :], ps[:, bk, :], G1[:, oc:oc + 1], xt[b][:, oc, :], ALU.mult, ALU.add,
                        [('ps', bk), ('xt', 0), 'modT'], [('xt', 0)])
                dma('pool', xo[:, :, tsl], xt[b], [('xt', 0)], [('x', T)], 'xs0')
                for k in range(8):
                    act(nxsq[:, k, :], xt[b][:, k, :], AF.Square, [('xt', 0)], [('nxsq', k)])
                bk = P.nextbank()
                mmgroup(ps[:, bk, :], [(ones_b, nxsq[:, k, :]) for k in range(8)],
                        [('nxsq', k) for k in range(8)], [('ps', bk)])
                act(nrs, ps[:, bk, :], AF.Sqrt, [('ps', bk)], ['nrs'], bias=eps_t, scale=1.0 / D)
                P.op('dve', lambda v: v.reciprocal(out=nrs, in_=nrs), ['nrs'], ['nrs'])
                for k in range(8):
                    stt('dve', ntmp[k % 2], xt[b][:, k, :], A2[:, k:k + 1], nrs, ALU.mult, ALU.mult,
                        [('xt', 0), 'nrs', 'prm'], [('ntmp', 0)])
                    act(hT[:, k, tsl], ntmp[k % 2], AF.Identity, [('ntmp', 0), 'modT'], [('hT', T)],
                        bias=B2[:, k:k + 1], scale=1.0)
            P.barrier()
            xsrc = xS
            if stop_after == 'merge':
                break

            for pas in range(2):
                rx_reset()
                Wua = wv(0, [8, 1408]); Wub = wv(11264, [8, 1408]); Wd = wv(22528, [11, 1024])
                for jj in range(11):
                    wload(Wua[:, :, jj * 128:(jj + 1) * 128], Wl['up'][:, :, pas * 1408 + jj * 128: pas * 1408 + (jj + 1) * 128],
                          key=('Wu', 0, jj))
                    wload(Wub[:, :, jj * 128:(jj + 1) * 128],
                          Wl['up'][:, :, DFF + pas * 1408 + jj * 128: DFF + pas * 1408 + (jj + 1) * 128], key=('Wu', 1, jj))
                wload(Wd, Wl['down'][pas * 1408:(pas + 1) * 1408, :].rearrange("(j p) n -> p j n", p=128), key=('Wu', 2))
                WB = 516
                Ua = [xf([WB]), xf([WB])]; Ub = [xf([WB]), xf([WB])]
                ca = [xf([520]), xf([520])]; cbb = [xf([520]), xf([520])]
                Ha = xf([11, 2]); Hb = xf([11, 2])
                G2b = [xb([11, 520]), xb([11, 520])]
                xw1 = xf([8, 520])
                memset('dve', Ha, 0.0, ['Ha']); memset('dve', Hb, 0.0, ['Hb'])
                for q in range(2):
                    memset('pool', Ua[q][:, 514:516], 0.0, [('Ua', q)])
                    memset('pool', Ub[q][:, 514:516], 0.0, [('Ub', q)])
                xo = fm(xS)

                def geom(T):
                    c0 = 2 if T == 0 else 1
                    c1 = 514 if T == NT - 1 else 513
                    return c0, c1, c1 - c0, T * 512 + c0 - 2

                def emit_up(T, jjs, pas=pas, Wua=Wua, Wub=Wub, Ua=Ua, Ub=Ub, Ha=Ha, Hb=Hb, ca=ca, cbb=cbb, G2b=G2b):
                    tsl = slice(T * 512, (T + 1) * 512)
                    c0, c1, wd, tok0 = geom(T)
                    G = G2b[T % 2]; gk = ('G', T % 2)
                    for jj in jjs:
                        q = jj % 2
                        ja = pas * 11 + jj; jb = 22 + ja
                        bka = P.nextbank()
                        mmgroup(ps[:, bka, :], [(Wua[:, m, jj * 128:(jj + 1) * 128], hT[:, m, tsl]) for m in range(8)],
                                [('hT', T), ('Wu', 0, jj)], [('ps', bka)])
                        bkb = P.nextbank()
                        mmgroup(ps[:, bkb, :], [(Wub[:, m, jj * 128:(jj + 1) * 128], hT[:, m, tsl]) for m in range(8)],
                                [('hT', T), ('Wu', 1, jj)], [('ps', bkb)])
                        for (U, H, bk, jx, cc, nm, e2) in ((Ua, Ha, bka, ja, ca, 'a', 'dve'), (Ub, Hb, bkb, jb, cbb, 'b', 'pool')):
                            uk = ('U' + nm, q); hk = 'H' + nm; ck = ('c' + nm, q)
                            copy('act', U[q][:, 2:514], ps[:, bk, :], [('ps', bk)], [uk])
                            copy('pool', U[q][:, 0:2], H[:, jj, :], [hk], [uk])
                            copy('pool', H[:, jj, :], U[q][:, 512:514], [uk], [hk])
                            act(cc[q][:, 0:wd], U[q][:, c0:c1], AF.Identity, [uk, 'small'], [ck],
                                bias=cb_v[:, li, jx:jx + 1], scale=cw_v[:, li, 1, jx:jx + 1])
                            stt('dve', cc[q][:, 0:wd], U[q][:, c0 - 1:c1 - 1], cw_v[:, li, 0, jx:jx + 1], cc[q][:, 0:wd],
                                ALU.mult, ALU.add, [uk, ck, 'small'], [ck])
                            stt('dve', cc[q][:, 0:wd], U[q][:, c0 + 1:c1 + 1], cw_v[:, li, 2, jx:jx + 1], cc[q][:, 0:wd],
                                ALU.mult, ALU.add, [uk, ck, 'small'], [ck])
                        act(ca[q][:, 0:wd], ca[q][:, 0:wd], AF.Gelu, [('ca', q)], [('ca', q)])
                        tt('dve', G[:, jj, 0:wd], ca[q][:, 0:wd], cbb[q][:, 0:wd], ALU.mult, [('ca', q), ('cb', q)], [gk])

                def emit_down(T, Wd=Wd, G2b=G2b, xw1=xw1):
                    c0, c1, wd, tok0 = geom(T)
                    G = G2b[T % 2]; gk = ('G', T % 2)
                    dma('sp', xw1[:, :, 0:wd], xo[:, :, tok0:tok0 + wd], [('xwin', T)], ['xw'], 'xw0')
                    for oc in range(8):
                        for (s0, s1_) in ((0, min(wd, 512)),) + (((512, wd),) if wd > 512 else ()):
                            bk = P.nextbank()
                            mmgroup(ps[:, bk, 0:s1_ - s0],
                                    [(Wd[:, jj, oc * 128:(oc + 1) * 128], G[:, jj, s0:s1_]) for jj in range(11)],
                                    [gk, ('Wu', 2)], [('ps', bk)])
                            stt('dve', xw1[:, oc, s0:s1_], ps[:, bk, 0:s1_ - s0], G2[:, oc:oc + 1], xw1[:, oc, s0:s1_],
                                ALU.mult, ALU.add, [('ps', bk), 'xw', 'modT'], ['xw'])
                    dma('sp', xo[:, :, tok0:tok0 + wd], xw1[:, :, 0:wd], ['xw'], [('xwin', T)], 'xws0')

                for T in range(NT):
                    emit_up(T, range(0, 4))
                    if T > 0:
                        emit_down(T - 1)
                    emit_up(T, range(4, 11))
                emit_down(NT - 1)
                if pas == 1:
                    P.barrier()

        if final and stop_after is None:
            rx_reset()
            xt = [xf([8, 512]), xf([8, 512])]
            xsq = xb([8, 512])
            rs = [xf([512]), xf([512])]
            xv = fm(xsrc); yv = fm(yout)
            for t in range(NT):
                b = t % 2
                dma('sp', xt[b], xv[:, :, t * 512:(t + 1) * 512], [('x', t)], [('xt', b)], f'xt{b}')
                for k in range(8):
                    act(xsq[:, k, :], xt[b][:, k, :], AF.Square, [('xt', b)], [('xsq', k)])
                bk = P.nextbank()
                mmgroup(ps[:, bk, :], [(ones_b, xsq[:, k, :]) for k in range(8)], [('xsq', k) for k in range(8)], [('ps', bk)])
                act(rs[b], ps[:, bk, :], AF.Sqrt, [('ps', bk)], [('rs', b)], bias=eps_t, scale=1.0 / D)
                P.op('dve', lambda v, o=rs[b]: v.reciprocal(out=o, in_=o), [('rs', b)], [('rs', b)])
                for k in range(8):
                    stt('dve', xt[b][:, k, :], xt[b][:, k, :], fing_t[:, k:k + 1], rs[b], ALU.mult, ALU.mult,
                        [('xt', b), ('rs', b), 'small', ('xsq', k)], [('xt', b)])
                dma('sp', yv[:, :, t * 512:(t + 1) * 512], xt[b], [('xt', b)], [('y', t)], f'ys{b}')
        else:
            rx_reset()
            xt = [xf([8, 512]), xf([8, 512])]
            xv = fm(xsrc); yv = fm(yout)
            for t in range(NT):
                b = t % 2
                dma('sp', xt[b], xv[:, :, t * 512:(t + 1) * 512], [('x', t)], [('xt', b)], f'xt{b}')
                dma('sp', yv[:, :, t * 512:(t + 1) * 512], xt[b], [('xt', b)], [('y', t)], f'ys{b}')
        P.barrier()

        import os as _os
        _xpe = int(_os.environ.get('XTRA_PE', '0')); _xdve = int(_os.environ.get('XTRA_DVE', '0'))
        if _xpe:
            def fn(t):
                for i in range(_xpe):
                    ins = t.matmul(ps[:, 0, :], lhsT=ident_b, rhs=RX[:, 0:512], start=True, stop=True)
                return ins
            P.op('pe', fn, (), [('ps', 0)])
        if _xdve:
            def fn(v):
                for i in range(_xdve):
                    ins = v.memset(stat[:, 0:1], 0.0)
                return ins
            P.op('dve', fn, (), ['junkx'])
        _xdma = int(_os.environ.get('XTRA_DMA', '0'))
        if _dmy is not None:
            dma('sp', stat[:, 0:8], _dmy[:, 219000:219008], (), ['junkd'], 'junkd')
        if _xdma:
            rx_reset()
            xtt = xf([8, 512])
            for i in range(_xdma):
                dma('sp', xtt, fm(xin)[:, :, (i % 8) * 512:(i % 8 + 1) * 512], (), ['xtt'], 'xtt')
            P.barrier()
        P.finalize(nc, es)
        block = es.enter_context(nc.Block())

        @block.tensor
        def _(t):
            P.emit(t, 'pe')

        @block.scalar
        def _(a):
            P.emit(a, 'act')

        @block.vector
        def _(v):
            P.emit(v, 'dve')

        @block.gpsimd
        def _(g):
            P.emit(g, 'pool')

        @block.sync
        def _(s):
            P.emit(s, 'sp')
    return nc


_CACHE = {}


def _get_nc(key, *args, **kw):
    if key not in _CACHE:
        _CACHE[key] = build(*args, **kw)
    return _CACHE[key]


def _layer_inputs(inp, layers):
    f = lambda a: np.ascontiguousarray(np.asarray(a, dtype=np.float32))
    g = lambda k: np.asarray(inp[k], dtype=np.float32)
    d = {}
    adab = np.stack([g('ada_b')[l].reshape(48, 128).T for l in layers], axis=1).reshape(128, -1)
    n1g = np.stack([g('norm1_g')[l].reshape(8, 128).T for l in layers], axis=1).reshape(128, -1)
    n2g = np.stack([g('norm2_g')[l].reshape(8, 128).T for l in layers], axis=1).reshape(128, -1)
    cw = np.stack([g('conv_w')[l].reshape(3, 44, 128).transpose(2, 0, 1) for l in layers], axis=1).reshape(128, -1)
    cb = np.stack([g('conv_b')[l].reshape(44, 128).T for l in layers], axis=1).reshape(128, -1)
    fing = g('final_g').reshape(8, 128).T
    d['smalls'] = f(np.concatenate([adab, n1g, n2g, cw, cb, fing], axis=1))
    for i, l in enumerate(layers):
        d[f'wp{i}'] = f(np.concatenate([g('w_in')[l], g('w_fourier')[l], g('w_ret')[l], g('w_out')[l],
                                        g('ffn_up')[l], g('ada_w')[l], g('w_in')[l][:, :1024].T], axis=1))
    d['downs'] = f(np.stack([g('ffn_down')[l] for l in layers], axis=0))
    C = make_consts()
    d['c_128'] = f(np.stack([C['ident'], C['cg'], C['sg'], C['g3c'], C['g3s']], axis=1))
    d['c_tab'] = f(np.concatenate([C['dmaskT'], C['tabf'], C['tabb'], C['decf'], C['decb']], axis=1))
    d['c_rot'] = f(np.stack([C['cosT'], C['sinT']], axis=1))
    d['c_tmat'] = f(C['tmat'])
    return d


FUSED = True


def kernel(**inp):
    x = np.asarray(inp['x'], dtype=np.float32)
    c = np.asarray(inp['c'], dtype=np.float32)
    B = x.shape[0]
    xT = [np.ascontiguousarray(x[b].T) for b in range(B)]
    cvs = [np.ascontiguousarray(c[b].reshape(8, 128).T) for b in range(B)]
    if FUSED:
        nc = _get_nc('fused', [0, 1, 2, 3], True)
        shared = _layer_inputs(inp, [0, 1, 2, 3])
        maps = [dict(shared, xT=xT[b], cvec=cvs[b]) for b in range(B)]
        res = run_bass_kernel_spmd(nc, maps, core_ids=list(range(B)))
        outs = [res.results[b]['yT'] for b in range(B)]
    else:
        cur = xT
        for l in range(4):
            last = (l == 3)
            nc = _get_nc('last' if last else 'layer', [0], last)
            shared = _layer_inputs(inp, [l])
            maps = [dict(shared, xT=cur[b], cvec=cvs[b]) for b in range(B)]
            res = run_bass_kernel_spmd(nc, maps, core_ids=list(range(B)))
            cur = [res.results[b]['yT'] for b in range(B)]
        outs = cur
    return np.stack([np.ascontiguousarray(o.T) for o in outs], axis=0).astype(np.float32)
```
